# Optimizing a Trainium2 kernel written in Bass

```python
import jax, jax.numpy as jnp
from jax import lax
import numpy as np

D_MODEL = 2048
BATCH = 8
SEQ = 4096
DEPTH = 4

CTX_LEN = 256
GRID_W = 64
MIX_W = 512
N_BRANCH = 4
GROUP_DIM = 128
N_GROUPS = MIX_W // GROUP_DIM
RET_CHUNK = 128
SGU_CHUNK = 128
CONV_W = 3
D_FF = -(-8 * D_MODEL // (3 * 256)) * 256
ROPE_BASE = 10000.0
RET_DECAY_EXP0 = 5
EPS = 1e-6

Q_OFF = 0
K_OFF = MIX_W
V_OFF = 2 * MIX_W
G_OFF = 3 * MIX_W
F_OFF = 4 * MIX_W
U_OFF = 5 * MIX_W
SC_OFF = 7 * MIX_W
GATE_OFF = 10 * MIX_W
IN_W = GATE_OFF + N_BRANCH * D_MODEL

kernel_name = "hybrid_retention_fourier_sgu_conv_dit"


def rmsnorm(x, g):
    xf = x.astype(jnp.float32)
    y = xf * lax.rsqrt(jnp.mean(xf * xf, axis=-1, keepdims=True) + EPS)
    return (y * g.astype(jnp.float32)).astype(x.dtype)


def group_rms(z):
    zf = z.astype(jnp.float32)
    return zf * lax.rsqrt(jnp.mean(zf * zf, axis=-1, keepdims=True) + EPS)


def modulate(h, shift, scale):
    return h * (1 + scale) + shift


def adaln(cond, w, b):
    m = jax.nn.silu(cond) @ w + b
    return jnp.split(m, 6, axis=-1)


def _rotate(x, pos):
    n = x.shape[-1]
    freqs = ROPE_BASE ** (-jnp.arange(0, n, 2, dtype=jnp.float32) / n)
    ang = pos.astype(jnp.float32)[:, None] * freqs[None, :]
    cos = jnp.cos(ang)[:, None, :].astype(x.dtype)
    sin = jnp.sin(ang)[:, None, :].astype(x.dtype)
    x1, x2 = x[..., : n // 2], x[..., n // 2:]
    return jnp.concatenate([x1 * cos - x2 * sin, x1 * sin + x2 * cos], axis=-1)


def rope_2d(x):
    L = x.shape[1]
    t = jnp.arange(L)
    h = x.shape[-1] // 2
    return jnp.concatenate([_rotate(x[..., :h], t // GRID_W), _rotate(x[..., h:], t % GRID_W)], axis=-1)


def retention_scan(q, k, v, log_g, s0):
    Bsz, L, H, d = q.shape
    C = RET_CHUNK
    n = L // C
    lg = log_g.astype(jnp.float32)
    qc = q.reshape(Bsz, n, C, H, d)
    kc = k.reshape(Bsz, n, C, H, d)
    vc = v.reshape(Bsz, n, C, H, d)
    i = jnp.arange(C, dtype=jnp.float32)
    rel = i[:, None] - i[None, :]
    decay = jnp.where(rel >= 0, jnp.exp(lg[:, None, None] * jnp.maximum(rel, 0.0)[None]), 0.0)
    scores = jnp.einsum('bnihd,bnjhd->bnhij', qc, kc) * decay[None, None]
    inner = jnp.einsum('bnhij,bnjhd->bnihd', scores, vc)
    k_w = jnp.exp(lg[None, :] * (C - 1 - i)[:, None])
    ds = jnp.einsum('bnjhd,bnjhe->nbhde', kc * k_w[None, None, :, :, None], vc)
    chunk_decay = jnp.exp(lg * C)[None, :, None, None]

    def step(s, ds_n):
        return chunk_decay * s + ds_n, s

    _, s_prev = lax.scan(step, s0.astype(jnp.float32), ds)
    q_w = jnp.exp(lg[None, :] * (i + 1)[:, None])
    cross = jnp.einsum('bnihd,nbhde->bnihe', qc * q_w[None, None, :, :, None], s_prev)
    return (inner + cross).reshape(Bsz, L, H, d)


def retention_state(k, v, log_g):
    L = k.shape[1]
    lg = log_g.astype(jnp.float32)
    w = jnp.exp(lg[None, :] * (L - 1 - jnp.arange(L, dtype=jnp.float32))[:, None])
    return jnp.einsum('blhd,blhe->bhde', k * w[None, :, :, None], v)


def bidir_retention(q, k, v, log_g2, s0_f, s0_b):
    fwd = retention_scan(q, k, v, log_g2[0], s0_f)
    bwd = retention_scan(jnp.flip(q, 1), jnp.flip(k, 1), jnp.flip(v, 1), log_g2[1], s0_b)
    return fwd + jnp.flip(bwd, 1)


def ctx_states(kv, log_g2):
    Bsz, L, _ = kv.shape
    k = kv[..., :MIX_W].astype(jnp.float32).reshape(Bsz, L, N_GROUPS, GROUP_DIM) * GROUP_DIM ** -0.5
    v = kv[..., MIX_W:].astype(jnp.float32).reshape(Bsz, L, N_GROUPS, GROUP_DIM)
    s_f = retention_state(k, v, log_g2[0])
    s_b = retention_state(jnp.flip(k, 1), jnp.flip(v, 1), log_g2[1])
    return s_f, s_b


def fourier_mix(z):
    Bsz, L, _ = z.shape
    zg = z.astype(jnp.float32).reshape(Bsz, L, N_GROUPS, GROUP_DIM)
    y = jnp.fft.fftn(zg, axes=(1, 3), norm="ortho").real
    return y.reshape(Bsz, L, MIX_W).astype(z.dtype)


def spatial_gating(z, w_s, b_s):
    Bsz, L, _ = z.shape
    z = jax.nn.gelu(z)
    u, v = z[..., :MIX_W], z[..., MIX_W:]
    vg = v.astype(jnp.float32).reshape(Bsz, L // SGU_CHUNK, SGU_CHUNK, N_GROUPS, GROUP_DIM)
    mu = jnp.mean(vg, axis=-1, keepdims=True)
    var = jnp.mean(jnp.square(vg - mu), axis=-1, keepdims=True)
    vg = (vg - mu) * lax.rsqrt(var + EPS)
    s = jnp.einsum('bnpgc,gqp->bnqgc', vg, w_s.astype(jnp.float32)) \
        + jnp.transpose(b_s.astype(jnp.float32))[None, None, :, :, None]
    return (u.astype(jnp.float32) * s.reshape(Bsz, L, MIX_W)).astype(z.dtype)


def conv3(y, w):
    C = y.shape[-1]
    return lax.conv_general_dilated(y, w[:, None, :].astype(y.dtype), window_strides=(1,),
                                    padding=((CONV_W // 2, CONV_W // 2),),
                                    dimension_numbers=('NWC', 'WIO', 'NWC'), feature_group_count=C)


def short_conv_mix(z, w, is_latent):
    Bsz, L, _ = z.shape
    b, cg, xv = z[..., :MIX_W], z[..., MIX_W:2 * MIX_W], z[..., 2 * MIX_W:]
    y = cg * xv
    if is_latent:
        rows = L // GRID_W
        y = conv3(y.reshape(Bsz * rows, GRID_W, MIX_W), w).reshape(Bsz, L, MIX_W)
    else:
        y = conv3(y, w)
    return b * y


def token_mix(proj, log_g2, conv_w, sgu_w, sgu_b, w_branch, w_out, s0_f, s0_b, is_latent):
    Bsz, L, _ = proj.shape
    heads = lambda z: z.astype(jnp.float32).reshape(Bsz, L, N_GROUPS, GROUP_DIM)
    q = heads(proj[..., Q_OFF:Q_OFF + MIX_W])
    k = heads(proj[..., K_OFF:K_OFF + MIX_W]) * GROUP_DIM ** -0.5
    v = heads(proj[..., V_OFF:V_OFF + MIX_W])
    g = proj[..., G_OFF:G_OFF + MIX_W]
    if is_latent:
        q, k = rope_2d(q), rope_2d(k)
    ret = bidir_retention(q, k, v, log_g2, s0_f, s0_b)
    y_a = jax.nn.silu(g) * group_rms(ret).reshape(Bsz, L, MIX_W).astype(g.dtype)
    y_b = fourier_mix(proj[..., F_OFF:F_OFF + MIX_W])
    y_c = spatial_gating(proj[..., U_OFF:U_OFF + 2 * MIX_W], sgu_w, sgu_b)
    y_d = short_conv_mix(proj[..., SC_OFF:SC_OFF + 3 * MIX_W], conv_w, is_latent)
    branches = jnp.stack([y_a, y_b, y_c, y_d], axis=2).astype(proj.dtype)
    gates = jax.nn.sigmoid(proj[..., GATE_OFF:].reshape(Bsz, L, N_BRANCH, D_MODEL))
    merged = jnp.sum(gates * jnp.einsum('blnm,nmd->blnd', branches, w_branch), axis=2)
    return merged @ w_out


def swiglu(h, w1, w2):
    a = h @ w1
    return (jax.nn.silu(a[..., :D_FF]) * a[..., D_FF:]) @ w2


def setup_inputs(seed: int = 0) -> dict:
    key = jax.random.key(seed)
    ks = jax.random.split(key, 16)
    f32 = jnp.float32
    nrm = lambda k, s: jax.random.normal(k, s, f32)
    base_lg = jnp.log1p(-jnp.exp2(-(RET_DECAY_EXP0 + jnp.arange(N_GROUPS, dtype=f32))))
    return {
        "x": nrm(ks[0], (BATCH, SEQ, D_MODEL)),
        "c": nrm(ks[1], (BATCH, D_MODEL)),
        "ctx": nrm(ks[2], (BATCH, CTX_LEN, D_MODEL)),
        "c_ctx": nrm(ks[3], (D_MODEL,)),
        "ada_w": nrm(ks[4], (DEPTH, D_MODEL, 6 * D_MODEL)) * (0.5 * D_MODEL ** -0.5),
        "ada_b": nrm(ks[5], (DEPTH, 6 * D_MODEL)) * 0.01,
        "norm_g": 1.0 + 0.02 * nrm(ks[6], (DEPTH, 4, D_MODEL)),
        "w_in": nrm(ks[7], (DEPTH, D_MODEL, IN_W)) * D_MODEL ** -0.5,
        "ret_log_decay": base_lg[None, None, :] * (1.0 + 0.05 * nrm(ks[8], (DEPTH, 2, N_GROUPS))),
        "conv_w": nrm(ks[9], (DEPTH, CONV_W, MIX_W)) * CONV_W ** -0.5,
        "sgu_w": nrm(ks[10], (DEPTH, N_GROUPS, SGU_CHUNK, SGU_CHUNK)) * SGU_CHUNK ** -0.5,
        "sgu_b": 0.02 * nrm(ks[11], (DEPTH, N_GROUPS, SGU_CHUNK)),
        "w_branch": nrm(ks[12], (DEPTH, N_BRANCH, MIX_W, D_MODEL)) * MIX_W ** -0.5,
        "w_out": nrm(ks[13], (DEPTH, D_MODEL, D_MODEL)) * D_MODEL ** -0.5,
        "ffn_w_in": nrm(ks[14], (DEPTH, D_MODEL, 2 * D_FF)) * D_MODEL ** -0.5,
        "ffn_w_out": nrm(ks[15], (DEPTH, D_FF, D_MODEL)) * D_FF ** -0.5,
    }


def reference(x, c, ctx, c_ctx, ada_w, ada_b, norm_g, w_in, ret_log_decay, conv_w, sgu_w, sgu_b,
              w_branch, w_out, ffn_w_in, ffn_w_out):
    for l in range(DEPTH):
        last = l == DEPTH - 1
        sx1, cx1, gx1, sx2, cx2, gx2 = [m[:, None, :] for m in adaln(c, ada_w[l], ada_b[l])]
        sc1, cc1, gc1, sc2, cc2, gc2 = adaln(c_ctx, ada_w[l], ada_b[l])

        hc = modulate(rmsnorm(ctx, norm_g[l, 0]), sc1, cc1)
        if last:
            kv_c = hc @ w_in[l][:, K_OFF:V_OFF + MIX_W]
        else:
            proj_c = hc @ w_in[l]
            kv_c = proj_c[..., K_OFF:V_OFF + MIX_W]
        s_f, s_b = ctx_states(kv_c, ret_log_decay[l])

        hx = modulate(rmsnorm(x, norm_g[l, 0]), sx1, cx1)
        mix_x = token_mix(hx @ w_in[l], ret_log_decay[l], conv_w[l], sgu_w[l], sgu_b[l],
                          w_branch[l], w_out[l], s_f, s_b, True)
        x = x + gx1 * rmsnorm(mix_x, norm_g[l, 1])
        hx = modulate(rmsnorm(x, norm_g[l, 2]), sx2, cx2)
        x = x + gx2 * rmsnorm(swiglu(hx, ffn_w_in[l], ffn_w_out[l]), norm_g[l, 3])

        if not last:
            zero_s = jnp.zeros_like(s_f)
            mix_c = token_mix(proj_c, ret_log_decay[l], conv_w[l], sgu_w[l], sgu_b[l],
                              w_branch[l], w_out[l], zero_s, zero_s, False)
            ctx = ctx + gc1 * rmsnorm(mix_c, norm_g[l, 1])
            hc = modulate(rmsnorm(ctx, norm_g[l, 2]), sc2, cc2)
            ctx = ctx + gc2 * rmsnorm(swiglu(hc, ffn_w_in[l], ffn_w_out[l]), norm_g[l, 3])
    return x
```

```python
import numpy as np
import ml_dtypes
import concourse.bass as bass
import concourse.mybir as mybir
from concourse.bass_utils import run_bass_kernel_spmd

F32 = mybir.dt.float32
BF16 = mybir.dt.bfloat16
U8 = mybir.dt.uint8
AF = mybir.ActivationFunctionType
ALU = mybir.AluOpType
AX = mybir.AxisListType

D = 2048
SEQ = 4096
CTX = 256
DEPTH = 4
MIXW = 512
DFF = 5632
INW = 13312
EPS = 1e-6
NCORES = 8

ENGS = ['pe', 'act', 'dve', 'pool', 'sp']
EIDX = {e: i for i, e in enumerate(ENGS)}
NDMA_SEM = 10
DMA_ENGS = ['sp', 'pool', 'act']
NK = len(ENGS) + len(DMA_ENGS) * NDMA_SEM


class Op:
    __slots__ = ('eng', 'fn', 'reads', 'writes', 'dma', 'deps', 'sig', 'sem', 'val', 'idx', 'clock', 'kidx')

    def __init__(s, eng, fn, reads, writes, dma):
        s.eng = eng; s.fn = fn; s.reads = reads; s.writes = writes; s.dma = dma
        s.deps = []; s.sig = False; s.sem = None; s.val = 0; s.clock = None; s.kidx = EIDX[eng]


class Prog:
    def __init__(self, nc):
        self.nc = nc
        self.ops = []
        self.lastw = {}
        self.readers = {}
        self.last_on = {e: None for e in ENGS}
        self.bar_deps = []
        self.bar_pending = set()
        self.dma_cnt = {e: 0 for e in DMA_ENGS}
        self.dma_last = {}

    def op(self, eng, fn, reads=(), writes=(), dma=False):
        pr_ = [r for r in reads if r == 'ps0' or (isinstance(r, tuple) and r[0] == 'ps')]
        if pr_:
            writes = tuple(writes) + tuple(r for r in pr_ if r not in writes)
        o = Op(eng, fn, tuple(reads), tuple(writes), dma)
        o.idx = len(self.ops)
        deps = set()
        if eng in self.bar_pending:
            deps.update(self.bar_deps); self.bar_pending.discard(eng)
        lastw = self.lastw
        for r in o.reads:
            w = lastw.get(r)
            if w is not None: deps.add(w)
        for r in o.writes:
            w = lastw.get(r)
            if w is not None: deps.add(w)
            rs = self.readers.get(r)
            if rs: deps.update(rs.values())
        if dma:
            k = self.dma_cnt[eng]; self.dma_cnt[eng] += 1
            slot = k % NDMA_SEM
            o.kidx = len(ENGS) + DMA_ENGS.index(eng) * NDMA_SEM + slot
            o.val = 16 * (k // NDMA_SEM + 1)
            prev = self.dma_last.get(o.kidx)
            if prev is not None: deps.add(prev)
            self.dma_last[o.kidx] = o.idx
            o.sig = True
        ops = self.ops
        for d in deps:
            p = ops[d]
            if p.eng == eng and not p.dma and not dma:
                if eng == 'pe': continue
            o.deps.append(d)
        rk = ('d', o.idx) if dma else eng
        for r in o.reads:
            self.readers.setdefault(r, {})[rk] = o.idx
        for r in o.writes:
            lastw[r] = o.idx
            self.readers[r] = {}
        ops.append(o)
        self.last_on[eng] = o.idx
        return o

    def barrier(self):
        deps = set()
        for e in ENGS:
            if self.last_on[e] is not None: deps.add(self.last_on[e])
        deps.update(self.dma_last.values())
        self.bar_deps = sorted(deps)
        self.bar_pending = set(ENGS)
        self.lastw = {}; self.readers = {}

    def finish(self):
        self.barrier()
        self.op('sp', lambda e: e.nop())

    def emit(self, new_sem):
        ops = self.ops
        for o in ops:
            for d in o.deps: ops[d].sig = True
        sems = [new_sem(f"k{i}") for i in range(NK)]
        cnt = [0] * len(ENGS)
        for o in ops:
            if not o.dma:
                if o.sig: cnt[o.kidx] += 1
                o.val = cnt[o.kidx]
            o.sem = sems[o.kidx]
        self.sig_counts = cnt
        know = {e: [0] * NK for e in ENGS}
        streams = {e: [] for e in ENGS}
        for o in ops:
            kc = know[o.eng]
            waits = {}
            for d in o.deps:
                p = ops[d]
                if kc[p.kidx] >= p.val: continue
                if waits.get(p.kidx, 0) < p.val: waits[p.kidx] = p.val
            for d in o.deps:
                p = ops[d]
                if kc[p.kidx] >= p.val: continue
                pc = p.clock
                if pc is not None:
                    for i in range(NK):
                        if pc[i] > kc[i]: kc[i] = pc[i]
                if kc[p.kidx] < p.val: kc[p.kidx] = p.val
            if o.sig:
                c2 = list(kc)
                if c2[o.kidx] < o.val: c2[o.kidx] = o.val
                o.clock = c2
            streams[o.eng].append((o, [(sems[k], v) for k, v in waits.items()]))
        return streams


def run_streams(nc, streams):
    engmap = {'pe': 'tensor', 'act': 'scalar', 'dve': 'vector', 'pool': 'gpsimd', 'sp': 'sync'}
    with nc.Block() as block:
        for e in ENGS:
            lst = streams[e]

            def body(eng, lst=lst):
                for o, waits in lst:
                    for s, v in waits:
                        eng.wait_ge(s, v)
                    ins = o.fn(eng)
                    if o.sig:
                        ins.then_inc(o.sem, 16 if o.dma else 1)
            getattr(block, engmap[e])(body)


ARENA_BYTES = 196 * 1024


class Rot:
    def __init__(self, name, aps):
        self.name = name; self.aps = aps; self.i = 0

    def next(self):
        k = self.i % len(self.aps); self.i += 1
        return self.aps[k], (self.name, k)


class Builder:
    def __init__(self, depth=DEPTH, taps=(), stop_after=None):
        self.depth = depth
        self.taps = set(taps)
        self.stop_after = stop_after
        nc = self.nc = bass.Bass("TRN2", target_bir_lowering=False)
        self.P = Prog(nc)
        self.din = {}
        self._uid = 0

    def inp(self, name, shape, dt):
        t = self.nc.dram_tensor(name, list(shape), dt, kind="ExternalInput").ap()
        self.din[name] = t
        return t

    def scratch(self, name, shape, dt):
        kind = "ExternalOutput" if name in self.taps else "Internal"
        return self.nc.dram_tensor(name, list(shape), dt, kind=kind).ap()

    def arena_reset(self):
        self.aoff = 0

    def alloc(self, shape, dt):
        esz = 4 if dt == F32 else 2
        n = int(np.prod(shape[1:])) * esz
        n_al = (n + 63) // 64 * 64
        assert self.aoff + n_al <= ARENA_BYTES, (self.aoff, n_al)
        ap = self.arena[:, self.aoff:self.aoff + n].bitcast(dt)
        self.aoff += n_al
        if len(shape) == 3:
            ap = ap.rearrange("p (a b) -> p a b", a=shape[1])
        elif len(shape) == 4:
            ap = ap.rearrange("p (a b c) -> p a b c", a=shape[1], b=shape[2])
        return ap

    def rot(self, name, n, shape, dt):
        return Rot(name, [self.alloc(shape, dt) for _ in range(n)])

    def uid(self, s):
        self._uid += 1
        return f"{s}{self._uid}"

    def dma(self, q, out, in_, reads, writes, **kw):
        self.P.op(q, lambda e: e.dma_start(out=out, in_=in_, **kw), reads=reads, writes=writes, dma=True)

    def mm(self, out, lhsT, rhs, start, stop, reads, writes):
        self.P.op('pe', lambda e: e.matmul(out, lhsT, rhs, start=start, stop=stop), reads=reads, writes=writes)

    def act(self, out, in_, func, reads, writes, scale=1.0, bias=None, accum_out=None):
        kw = {}
        if bias is not None: kw['bias'] = bias
        if accum_out is not None: kw['accum_out'] = accum_out
        self.P.op('act', lambda e: e.activation(out, in_, func, scale=scale, **kw), reads=reads, writes=writes)

    def tt(self, eng, out, in0, in1, op, reads, writes):
        self.P.op(eng, lambda e: e.tensor_tensor(out, in0, in1, op), reads=reads, writes=writes)

    def ts(self, eng, out, in0, s1, s2, op0, op1, reads, writes):
        if op1 is None:
            self.P.op(eng, lambda e: e.tensor_scalar(out, in0, s1, None, op0), reads=reads, writes=writes)
        else:
            self.P.op(eng, lambda e: e.tensor_scalar(out, in0, s1, s2, op0, op1), reads=reads, writes=writes)

    def stt(self, eng, out, in0, scalar, in1, op0, op1, reads, writes):
        self.P.op(eng, lambda e: e.scalar_tensor_tensor(out, in0, scalar, in1, op0, op1), reads=reads, writes=writes)

    def copy(self, eng, out, in_, reads, writes):
        if eng == 'act':
            self.act(out, in_, AF.Copy, reads, writes)
        else:
            self.P.op(eng, lambda e: e.tensor_copy(out, in_), reads=reads, writes=writes)

    def rstd(self, out, ss, inv_n, reads, key):
        self.ts('dve', out, ss, inv_n, EPS, ALU.mult, ALU.add, reads, [key])
        self.act(out, out, AF.Sqrt, [key], [key])
        self.P.op('dve', lambda e: e.reciprocal(out, out), reads=[key], writes=[key])

    def build(self):
        nc = self.nc
        depth = self.depth
        xT = self.inp("xT", [D, SEQ], F32)
        ctxT = self.inp("ctxT", [D, CTX], F32)
        cc = self.inp("cc", [128, 16, 2], F32)
        DEPTH = self.depth
        ada_w = self.inp("ada_w", [DEPTH, D, 6 * D], F32)
        ada_b = self.inp("ada_b", [DEPTH, 128, 96], F32)
        norm_g = self.inp("norm_g", [128, 4 * 4 * 16], F32)
        w_in = self.inp("w_in", [DEPTH, D, INW], F32)
        lgb = self.inp("lgb", [128, 4 * 8], F32)
        conv_w = self.inp("conv_w", [128, 4 * 4 * 3], F32)
        sgu_wT = self.inp("sgu_wT", [DEPTH, 128, 4, 128], F32)
        sgu_bb = self.inp("sgu_bb", [DEPTH, 128, 4, 512], F32)
        w_branch = self.inp("w_branch", [DEPTH, 4 * MIXW, D], F32)
        w_out = self.inp("w_out", [DEPTH, 16, 128, D], F32)
        ffn_w_in = self.inp("ffn_w_in", [DEPTH, D, 2 * DFF], F32)
        ffn_w_out = self.inp("ffn_w_out", [DEPTH, 16, 128, DFF], F32)
        dconst = self.inp("dconst", [128, 6 * 128 + 4], F32)
        cmat = self.inp("cmat", [128, 3 * 128], BF16)
        dftc = self.inp("dftc", [128, 256], BF16)
        rope_l = self.inp("rope_l", [2, 128, SEQ], F32)
        rope_c = self.inp("rope_c", [2, 128, CTX], F32)
        dft_l = self.inp("dft_l", [2, SEQ, SEQ], BF16)
        dft_cx = self.inp("dft_cx", [2, CTX, CTX], BF16)
        outT = nc.dram_tensor("outT", [D, SEQ], F32, kind="ExternalOutput").ap()

        def mkstream(nm, T):
            s = dict(name=nm, T=T)
            s['qkT'] = self.scratch(f"{nm}_qkT", [1024, T], BF16)
            s['v_tok'] = self.scratch(f"{nm}_vtok", [T, 512], BF16)
            s['sgT'] = self.scratch(f"{nm}_sgT", [512, T], BF16)
            s['fT'] = self.scratch(f"{nm}_fT", [512, T], BF16)
            s['uT'] = self.scratch(f"{nm}_uT", [512, T], BF16)
            s['vs_tok'] = self.scratch(f"{nm}_vstok", [T, 512], BF16)
            s['scT'] = self.scratch(f"{nm}_scT", [1536, T], BF16)
            s['gateT'] = self.scratch(f"{nm}_gateT", [8192, T], BF16)
            s['yT'] = self.scratch(f"{nm}_yT", [2048, T], BF16)
            s['mergedT'] = self.scratch(f"{nm}_mergedT", [2048, T], BF16)
            s['hT'] = self.scratch(f"{nm}_hT", [DFF, T], BF16)
            return s
        lat = mkstream("lat", SEQ)
        lat.update(j=0, RW=64, rope=rope_l, dft=dft_l, TB=2048, res_in=xT, res=outT)
        ctx = mkstream("ctx", CTX)
        ctxres = self.scratch("ctx_res", [D, CTX], F32)
        ctx.update(j=1, RW=CTX, rope=rope_c, dft=dft_cx, TB=CTX, res_in=ctxT, res=ctxres)
        self.s0 = self.scratch("s0", [8, 128, 128], F32)

        self.modv = nc.alloc_sbuf_tensor("modv", [128, 6, 16, 2], F32)
        self.ng = nc.alloc_sbuf_tensor("ng", [128, 4 * 4 * 16], F32)
        self.lg = nc.alloc_sbuf_tensor("lg", [128, 4 * 8], F32)
        self.cw = nc.alloc_sbuf_tensor("cw", [128, 4 * 4 * 3], F32)
        self.dc = nc.alloc_sbuf_tensor("dc", [128, 6 * 128 + 4], F32)
        self.cm = nc.alloc_sbuf_tensor("cm", [128, 3 * 128], BF16)
        self.scc = nc.alloc_sbuf_tensor("scc", [128, 16, 2], BF16)
        self.ccf = nc.alloc_sbuf_tensor("ccf", [128, 16, 2], F32)
        self.arena = nc.alloc_sbuf_tensor("arena", [128, ARENA_BYTES], U8)
        self.ps = [nc.alloc_psum_tensor(f"ps{i}", [128, 512], F32) for i in range(8)]
        self.ident = self.cm[:, 0:128]
        self.perm = self.cm[:, 128:256]
        self.ones = self.cm[:, 256:384]

        P = self.P
        for (dst, src, nm) in ((self.ng, norm_g, 'ng'), (self.lg, lgb, 'lg'), (self.cw, conv_w, 'cw'),
                               (self.dc, dconst, 'dc'), (self.cm, cmat, 'cm')):
            self.dma('sp', dst[:], src[:], [], [nm])
        self.dma('sp', self.ccf[:], cc[:], [], ['ccf'])
        self.act(self.scc[:], self.ccf[:], AF.Silu, ['ccf'], ['scc'])
        P.barrier()

        W = dict(ada_w=ada_w, ada_b=ada_b, w_in=w_in, sgu_wT=sgu_wT, sgu_bb=sgu_bb, w_branch=w_branch,
                 w_out=w_out, ffn_w_in=ffn_w_in, ffn_w_out=ffn_w_out, dftc=dftc)
        self.W = W
        done = False
        for l in range(depth):
            last = (l == 3)
            self.stage_adaln(l)
            for st in (ctx, lat):
                def stop(nm):
                    return self.stop_after == st['name'] + '_' + nm
                if self.stop_after == 'adaln': done = True; break
                self.stage_gemm_in(l, st, which='inproj')
                if stop('inproj'): done = True; break
                self.stage_retention(l, st)
                if stop('ret'): done = True; break
                if st is ctx and last:
                    continue
                self.stage_fourier(l, st)
                if stop('fourier'): done = True; break
                self.stage_sgu(l, st)
                if stop('sgu'): done = True; break
                self.stage_conv(l, st)
                if stop('conv'): done = True; break
                self.stage_merge(l, st)
                if stop('merge'): done = True; break
                self.stage_outproj(l, st, which='mix')
                if stop('sub1'): done = True; break
                self.stage_gemm_in(l, st, which='ffn1')
                if stop('ffn1'): done = True; break
                self.stage_outproj(l, st, which='ffn2')
                if stop('ffn2'): done = True; break
                st['res_in'] = st['res']
            if done: break
        P.finish()
        streams = P.emit(lambda name: nc.alloc_semaphore(name))
        run_streams(nc, streams)
        return nc

    def stage_adaln(self, l):
        P = self.P
        self.arena_reset()
        wrot = self.rot('aw', 3, [128, 16, 512], BF16)
        ab = self.alloc([128, 96], F32)
        mod = self.alloc([128, 96, 2], F32)
        self.dma('sp', ab, self.W['ada_b'][l], [], ['ab'])
        pst = self.ps[0][:]
        psv = pst[:, 0:192].rearrange("p (a b) -> p a b", b=2)
        for c in range(24):
            wt, wk = wrot.next()
            self.dma('pool', wt, self.W['ada_w'][l][:, c * 512:(c + 1) * 512].rearrange("(k p) n -> p k n", p=128), [], [wk])
            for m in range(4):
                ct = c * 4 + m
                for k in range(16):
                    self.mm(pst[:, ct * 2:ct * 2 + 2], wt[:, k, m * 128:(m + 1) * 128], self.scc[:, k, :],
                            k == 0, k == 15, [wk, 'scc'], ['ps0'])
        for j in range(2):
            self.tt('dve', mod[:, :, j], psv[:, :, j], ab, ALU.add, ['ps0', 'ab'], ['mod'])
        ngl = self.ng[:, l * 64:(l + 1) * 64].rearrange("p (i k) -> p i k", i=4)
        mv = self.modv
        for j in range(2):
            self.stt('dve', mv[:, 0, :, j], mod[:, 16:32, j], 1.0, ngl[:, 0, :], ALU.add, ALU.mult, ['mod', 'ng'], ['modv'])
            self.copy('dve', mv[:, 1, :, j], mod[:, 0:16, j], ['mod'], ['modv'])
            self.tt('dve', mv[:, 2, :, j], mod[:, 32:48, j], ngl[:, 1, :], ALU.mult, ['mod', 'ng'], ['modv'])
            self.stt('dve', mv[:, 3, :, j], mod[:, 64:80, j], 1.0, ngl[:, 2, :], ALU.add, ALU.mult, ['mod', 'ng'], ['modv'])
            self.copy('dve', mv[:, 4, :, j], mod[:, 48:64, j], ['mod'], ['modv'])
            self.tt('dve', mv[:, 5, :, j], mod[:, 80:96, j], ngl[:, 3, :], ALU.mult, ['mod', 'ng'], ['modv'])
        if 'tap_modv' in self.taps:
            tap = self.scratch('tap_modv', [128, 192], F32)
            self.dma('sp', tap, mv[:].rearrange("p a b c -> p (a b c)"), ['modv'], ['tapm'])
        P.barrier()

    def stage_gemm_in(self, l, st, which):
        P = self.P
        T = st['T']; TB = st['TB']; j = st['j']
        TT = 256
        mi = 0 if which == 'inproj' else 3
        A = self.modv[:, mi, :, j]; S = self.modv[:, mi + 1, :, j]
        src = st['res_in'] if which == 'inproj' else st['res']
        for tb in range(T // TB):
            self.arena_reset()
            hx = self.alloc([128, 16, TB], BF16)
            xrot = self.rot('xin', 2, [128, 16, TT], F32)
            sqrot = self.rot('sq', 4, [128, TT], BF16)
            rsrot = self.rot('rs', 2, [128, TT], F32)
            tmrot = self.rot('tm', 3, [128, 512], F32)
            nw = 3 if which == 'inproj' else 4
            wrot = self.rot('w', nw, [128, 16, 512], BF16)
            orot = self.rot('ob', 3, [128, TB], BF16)
            otrot = self.rot('obt', 3, [128, 512], BF16)
            psi = [0]

            def nps():
                k = 1 + psi[0] % 7; psi[0] += 1
                return self.ps[k][:], ('ps', k)
            for t in range(TB // TT):
                t0 = tb * TB + t * TT
                xt, xk = xrot.next()
                self.dma('sp', xt, src[:, t0:t0 + TT].rearrange("(k p) n -> p k n", p=128), ['res'], [xk])
                ss = self.ps[0][:, 0:TT]
                for k in range(16):
                    sq, sk = sqrot.next()
                    self.act(sq, xt[:, k, :], AF.Square, [xk], [sk])
                    self.mm(ss, self.ones, sq, k == 0, k == 15, [sk], ['ps0'])
                rs, rk = rsrot.next()
                self.rstd(rs, ss, 1.0 / D, ['ps0'], rk)
                for k in range(16):
                    tm, tk = tmrot.next()
                    tmv = tm[:, 0:TT]
                    self.stt('dve', tmv, xt[:, k, :], A[:, k:k + 1], rs, ALU.mult, ALU.mult, [xk, rk, 'modv'], [tk])
                    self.act(hx[:, k, t * TT:(t + 1) * TT], tmv, AF.Identity, [tk, 'modv'], [('hx', t)], bias=S[:, k:k + 1])
            hxr = [('hx', t) for t in range(TB // TT)]
            import os as _os
            if _os.environ.get('KDBG_SKIP_GEMM'):
                P.barrier(); continue
            if which == 'inproj':
                wsrc = self.W['w_in'][l]
                plan = []
                plan.append((0, 'B', AF.Copy, st['qkT'], 0))
                plan.append((1, 'B', AF.Copy, st['qkT'], 512))
                plan.append((2, 'A', AF.Copy, st['v_tok'], 0))
                plan.append((4, 'B', AF.Copy, st['fT'], 0))
                for c in (7, 8, 9):
                    plan.append((c, 'B', AF.Copy, st['scT'], (c - 7) * 512))
                plan.append((3, 'B', AF.Silu, st['sgT'], 0))
                plan.append((5, 'B', AF.Gelu_apprx_tanh, st['uT'], 0))
                plan.append((6, 'A', AF.Gelu_apprx_tanh, st['vs_tok'], 0))
                for c in range(10, 26):
                    plan.append((c, 'B', AF.Sigmoid, st['gateT'], (c - 10) * 512))
                ei = 0
                if _os.environ.get('KDBG_COPYONLY'):
                    plan = [(c, ori, AF.Copy, dst, off) for (c, ori, func, dst, off) in plan]
                if _os.environ.get('KDBG_PLAN'):
                    plan = [plan[int(i_)] for i_ in _os.environ['KDBG_PLAN'].split(',')]
                if _os.environ.get('KDBG_NOA'):
                    plan = [p_ for p_ in plan if p_[1] == 'B']
                if _os.environ.get('KDBG_ONLYA'):
                    plan = [p_ for p_ in plan if p_[1] == 'A']
                for (c, ori, func, dst, off) in plan:
                    wt, wk = wrot.next()
                    self.dma('pool', wt, wsrc[:, c * 512:(c + 1) * 512].rearrange("(k p) n -> p k n", p=128), [], [wk])
                    if ori == 'B':
                        for m in range(4):
                            ob, ok = orot.next()
                            for t in range(TB // 512 if TB >= 512 else 1):
                                tw = min(512, TB)
                                pt, pk = nps()
                                for k in range(16):
                                    self.mm(pt[:, 0:tw], wt[:, k, m * 128:(m + 1) * 128], hx[:, k, t * tw:(t + 1) * tw],
                                            k == 0, k == 15, [wk] + hxr, [pk])
                                if func == AF.Copy and ei % 2 == 1:
                                    self.copy('dve', ob[:, t * tw:(t + 1) * tw], pt[:, 0:tw], [pk], [ok])
                                else:
                                    self.act(ob[:, t * tw:(t + 1) * tw], pt[:, 0:tw], func, [pk], [ok])
                                ei += 1
                            r0 = off + m * 128
                            self.dma('sp', dst[r0:r0 + 128, tb * TB:(tb + 1) * TB], ob, [ok], [self.uid('st')])
                    else:
                        for tt_ in range(TB // 128):
                            pt, pk = nps()
                            for k in range(16):
                                self.mm(pt, hx[:, k, tt_ * 128:(tt_ + 1) * 128], wt[:, k, :], k == 0, k == 15, [wk] + hxr, [pk])
                            ob, ok = otrot.next()
                            if func == AF.Copy and ei % 2 == 1:
                                self.copy('dve', ob, pt, [pk], [ok])
                            else:
                                self.act(ob, pt, func, [pk], [ok])
                            ei += 1
                            r0 = tb * TB + tt_ * 128
                            self.dma('sp', dst[r0:r0 + 128, :], ob, [ok], [self.uid('st')])
            else:
                wsrc = self.W['ffn_w_in'][l]
                for c in range(DFF // 512):
                    wa, wak = wrot.next()
                    self.dma('pool', wa, wsrc[:, c * 512:(c + 1) * 512].rearrange("(k p) n -> p k n", p=128), [], [wak])
                    wb, wbk = wrot.next()
                    self.dma('pool', wb, wsrc[:, DFF + c * 512:DFF + (c + 1) * 512].rearrange("(k p) n -> p k n", p=128), [], [wbk])
                    for m in range(4):
                        ob, ok = orot.next()
                        for t in range(TB // 512 if TB >= 512 else 1):
                            tw = min(512, TB)
                            pa, pak = nps()
                            for k in range(16):
                                self.mm(pa[:, 0:tw], wa[:, k, m * 128:(m + 1) * 128], hx[:, k, t * tw:(t + 1) * tw],
                                        k == 0, k == 15, [wak] + hxr, [pak])
                            pb, pbk = nps()
                            for k in range(16):
                                self.mm(pb[:, 0:tw], wb[:, k, m * 128:(m + 1) * 128], hx[:, k, t * tw:(t + 1) * tw],
                                        k == 0, k == 15, [wbk] + hxr, [pbk])
                            tm, tk = tmrot.next()
                            self.act(tm[:, 0:tw], pa[:, 0:tw], AF.Silu, [pak], [tk])
                            self.tt('dve', ob[:, t * tw:(t + 1) * tw], tm[:, 0:tw], pb[:, 0:tw], ALU.mult, [tk, pbk], [ok])
                        r0 = c * 512 + m * 128
                        self.dma('sp', st['hT'][r0:r0 + 128, tb * TB:(tb + 1) * TB], ob, [ok], [self.uid('st')])
            P.barrier()

    def stage_retention(self, l, st):
        P = self.P
        T = st['T']; NCH = T // 128
        is_ctx = st['name'] == 'ctx'
        self.arena_reset()
        dc = self.dc
        RELP = dc[:, 0:128]; MPs = dc[:, 128:256]; RELN = dc[:, 256:384]; MNs = dc[:, 384:512]
        IR1 = dc[:, 512:640]; CI = dc[:, 640:768]
        CJ1 = dc[:, 768:769]; JJ = dc[:, 769:770]; CC = dc[:, 770:771]
        SC = float(128 ** -0.5)
        cosT = self.alloc([128, T], F32); sinT = self.alloc([128, T], F32)
        self.dma('sp', cosT, st['rope'][0], [], ['cos'])
        self.dma('sp', sinT, st['rope'][1], [], ['sin'])
        mask = self.alloc([128, 4, 128], F32); mtmp = self.alloc([128, 128], F32)
        QWf = self.alloc([128, 4, 128], F32); QWb = self.alloc([128, 4, 128], F32)
        kw = self.alloc([128, 16], F32)
        for h in range(4):
            lgf = self.lg[:, l * 8 + h:l * 8 + h + 1]; lgbk = self.lg[:, l * 8 + 4 + h:l * 8 + 4 + h + 1]
            mk = ('mask', h)
            self.act(mask[:, h, :], RELP, AF.Exp, ['dc', 'lg'], [mk], scale=lgf)
            self.tt('dve', mask[:, h, :], mask[:, h, :], MPs, ALU.mult, [mk, 'dc'], [mk])
            self.act(mtmp, RELN, AF.Exp, ['dc', 'lg'], ['mtmp'], scale=lgbk)
            self.tt('dve', mtmp, mtmp, MNs, ALU.mult, ['mtmp', 'dc'], ['mtmp'])
            self.tt('dve', mask[:, h, :], mask[:, h, :], mtmp, ALU.add, [mk, 'mtmp'], [mk])
            self.act(QWf[:, h, :], IR1, AF.Exp, ['dc', 'lg'], [('qwf', h)], scale=lgf)
            self.ts('dve', QWf[:, h, :], QWf[:, h, :], SC, None, ALU.mult, None, [('qwf', h)], [('qwf', h)])
            self.act(QWb[:, h, :], CI, AF.Exp, ['dc', 'lg'], [('qwb', h)], scale=lgbk)
            self.ts('dve', QWb[:, h, :], QWb[:, h, :], SC, None, ALU.mult, None, [('qwb', h)], [('qwb', h)])
            self.act(kw[:, h * 4 + 0:h * 4 + 1], CJ1, AF.Exp, ['dc', 'lg'], [('kw', h)], scale=lgf)
            self.act(kw[:, h * 4 + 1:h * 4 + 2], JJ, AF.Exp, ['dc', 'lg'], [('kw', h)], scale=lgbk)
            self.act(kw[:, h * 4 + 2:h * 4 + 3], CC, AF.Exp, ['dc', 'lg'], [('kw', h)], scale=lgf)
            self.act(kw[:, h * 4 + 3:h * 4 + 4], CC, AF.Exp, ['dc', 'lg'], [('kw', h)], scale=lgbk)
        qraw = self.alloc([128, T], BF16); kraw = self.alloc([128, T], BF16)
        qr = self.alloc([128, T], BF16); kr = self.alloc([128, T], BF16)
        qf = self.alloc([128, NCH, 128], BF16); qb = self.alloc([128, NCH, 128], BF16)
        kft = self.alloc([128, NCH, 128], BF16); kbt = self.alloc([128, NCH, 128], BF16)
        vt = self.alloc([128, NCH, 128], BF16)
        sg = self.alloc([128, T], BF16)
        ya = self.alloc([128, T], BF16)
        Sb_all = self.alloc([128, NCH, 128], BF16)
        Sf = self.alloc([128, 128], F32); Sb = self.alloc([128, 128], F32)
        Sfb = self.rot('sfb', 2, [128, 128], BF16)
        ret_all = self.alloc([128, NCH, 128], F32)
        ssq = self.alloc([128, NCH], F32)
        sq_all = self.alloc([128, NCH, 128], F32)
        smrot = self.rot('sm', 3, [128, 128], BF16)
        rnrot = self.rot('rn', 3, [128, 128], BF16)
        t1rot = self.rot('t1', 2, [128, 512], F32)
        t2rot = self.rot('t2', 2, [128, 512], F32)
        psi = [0]

        def nps():
            k = psi[0] % 8; psi[0] += 1
            return self.ps[k][:], ('ps', k)
        import os as _os
        RS = int(_os.environ.get('KDBG_RET', '99'))
        for h in range(4):
            if RS < 2: break
            hk = lambda s: (s, h)
            self.dma('sp', qraw, st['qkT'][h * 128:(h + 1) * 128, :], [], ['qraw'])
            self.dma('sp', kraw, st['qkT'][512 + h * 128:512 + (h + 1) * 128, :], [], ['kraw'])
            self.dma('sp', vt, st['v_tok'][:, h * 128:(h + 1) * 128].rearrange("(n p) e -> p n e", p=128), [], ['vt'])
            self.dma('sp', sg, st['sgT'][h * 128:(h + 1) * 128, :], [], ['sg'])
            if is_ctx:
                self.P.op('dve', lambda e: e.memset(Sf, 0.0), writes=['Sf'])
                self.P.op('dve', lambda e: e.memset(Sb, 0.0), writes=['Sb'])
            else:
                self.dma('sp', Sf, self.s0[h], ['s0'], ['Sf'])
                self.dma('sp', Sb, self.s0[4 + h], ['s0'], ['Sb'])
            TW = min(512, T)
            for (raw, rk, dst, dk) in ((qraw, 'qraw', qr, 'qr'), (kraw, 'kraw', kr, 'kr')):
                for t in range(T // TW):
                    sl = slice(t * TW, (t + 1) * TW)
                    pt, pk = nps()
                    self.mm(pt[:, 0:TW], self.perm, raw[:, sl], True, True, [rk, 'cm'], [pk])
                    t1, t1k = t1rot.next(); t2, t2k = t2rot.next()
                    self.tt('pool', t1[:, 0:TW], raw[:, sl], cosT[:, sl], ALU.mult, [rk, 'cos'], [t1k])
                    self.tt('dve', t2[:, 0:TW], pt[:, 0:TW], sinT[:, sl], ALU.mult, [pk, 'sin'], [t2k])
                    self.tt('dve', dst[:, sl], t1[:, 0:TW], t2[:, 0:TW], ALU.add, [t1k, t2k], [(dk, t)])
            qrk = [('qr', t) for t in range(T // TW)]
            krk = [('kr', t) for t in range(T // TW)]
            if RS < 3: continue
            G = 4
            for n0 in range(0, NCH, G):
                g = min(G, NCH - n0)
                for (dst, QW, nm) in ((qf, QWf, 'qwf'), (qb, QWb, 'qwb')):
                    for n in range(n0, n0 + g):
                        self.tt('pool', dst[:, n, :], qr[:, n * 128:(n + 1) * 128], QW[:, h, :], ALU.mult,
                                qrk + [(nm, h)], [(nm + 'd', n)])
            if RS < 4: continue
            for n in range(NCH):
                pt, pk = nps()
                ptb = pt[:].bitcast(BF16)[:, 0:128]
                self.P.op('pe', lambda e, ptb=ptb, n=n: e.transpose(ptb, kr[:, n * 128:(n + 1) * 128], self.ident),
                          reads=krk + ['cm'], writes=[pk])
                self.act(kft[:, n, :], ptb, AF.Copy, [pk, ('kw', h)], [('kft', n)], scale=kw[:, h * 4:h * 4 + 1])
                self.ts('dve', kbt[:, n, :], ptb, kw[:, h * 4 + 1:h * 4 + 2], None, ALU.mult, None, [pk, ('kw', h)], [('kbt', n)])
            if RS < 5: continue
            for n in range(NCH - 1, -1, -1):
                self.copy('act', Sb_all[:, n, :], Sb, ['Sb'], [('sball', n)])
                pt, pk = nps()
                self.mm(pt[:, 0:128], kbt[:, n, :], vt[:, n, :], True, True, [('kbt', n), 'vt'], [pk])
                self.stt('dve', Sb, Sb, kw[:, h * 4 + 3:h * 4 + 4], pt[:, 0:128], ALU.mult, ALU.add, ['Sb', pk, ('kw', h)], ['Sb'])
            if RS < 6: continue
            for n in range(NCH):
                sl = slice(n * 128, (n + 1) * 128)
                FW = int(_os.environ.get('KDBG_FW', '99'))
                sfb, sfk = Sfb.next()
                self.copy('act', sfb, Sf, ['Sf'], [sfk])
                if FW < 2: continue
                pa, pak = nps()
                self.mm(pa[:, 0:128], kr[:, sl], qr[:, sl], True, True, qrk + krk, [pak])
                sm, smk = smrot.next()
                self.tt('dve', sm, pa[:, 0:128], mask[:, h, :], ALU.mult, [pak, ('mask', h)], [smk])
                if FW < 3: continue
                pr, prk = nps()
                self.mm(pr[:, 0:128], sm, vt[:, n, :], True, False, [smk, 'vt'], [prk])
                self.mm(pr[:, 0:128], qf[:, n, :], sfb, False, False, [('qwfd', n), sfk], [prk])
                self.mm(pr[:, 0:128], qb[:, n, :], Sb_all[:, n, :], False, True, [('qwbd', n), ('sball', n)], [prk])
                if FW < 4: continue
                self.copy('dve', ret_all[:, n, :], pr[:, 0:128], [prk], [('ret', n)])
                self.act(sq_all[:, n, :], ret_all[:, n, :], AF.Square, [('ret', n)], [('sqa', n)])
                if FW < 5: continue
                pd, pdk = nps()
                self.mm(pd[:, 0:128], kft[:, n, :], vt[:, n, :], True, True, [('kft', n), 'vt'], [pdk])
                self.stt('dve', Sf, Sf, kw[:, h * 4 + 2:h * 4 + 3], pd[:, 0:128], ALU.mult, ALU.add, ['Sf', pdk, ('kw', h)], ['Sf'])
            if RS < 7: continue
            if is_ctx:
                self.dma('sp', self.s0[h], Sf, ['Sf'], ['s0'])
                self.dma('sp', self.s0[4 + h], Sb, ['Sb'], ['s0'])
            if RS < 8: continue
            ssk = [('sqa', n) for n in range(NCH)]
            self.P.op('dve', lambda e: e.reduce_sum(ssq, sq_all, axis=AX.X), reads=ssk, writes=['rstd_all'])
            self.rstd(ssq, ssq, 1.0 / 128, ['rstd_all'], 'rstd_all')
            for n in range(NCH):
                rn, rnk = rnrot.next()
                self.act(rn, ret_all[:, n, :], AF.Copy, [('ret', n), 'rstd_all'], [rnk], scale=ssq[:, n:n + 1])
                pt, pk = nps()
                ptb = pt[:].bitcast(BF16)[:, 0:128]
                self.P.op('pe', lambda e, ptb=ptb, rn=rn: e.transpose(ptb, rn, self.ident), reads=[rnk, 'cm'], writes=[pk])
                self.tt('dve', ya[:, n * 128:(n + 1) * 128], ptb, sg[:, n * 128:(n + 1) * 128], ALU.mult, [pk, 'sg'], [('ya', n)])
            self.dma('sp', st['yT'][h * 128:(h + 1) * 128, :], ya, [('ya', n) for n in range(NCH)], ['yT'])
        P.barrier()

    def stage_fourier(self, l, st):
        P = self.P
        T = st['T']; NCH = T // 128
        self.arena_reset()
        fz = self.alloc([128, 4, T], BF16)
        Acs = self.alloc([128, NCH, 4, 256], BF16)
        dfc = self.alloc([128, 256], BF16)
        KT = 256
        trot = self.rot('dt', 2, [128, 2, NCH, KT], BF16)
        self.dma('sp', dfc, self.W['dftc'][:], [], ['dfc'])
        for g in range(4):
            self.dma('sp', fz[:, g, :], st['fT'][g * 128:(g + 1) * 128, :], [], [('fz', g)])
        psi = [0]

        def nps():
            k = psi[0] % 8; psi[0] += 1
            return self.ps[k][:], ('ps', k)
        ei = 0
        for n in range(NCH):
            for g2 in range(2):
                pt, pk = nps()
                for gg in range(2):
                    g = g2 * 2 + gg
                    self.mm(pt[:, gg * 256:(gg + 1) * 256], fz[:, g, n * 128:(n + 1) * 128], dfc, True, True, [('fz', g), 'dfc'], [pk])
                dst = Acs[:, n, g2 * 2:g2 * 2 + 2, :]
                src = pt[:].rearrange("p (a b) -> p a b", a=2)
                self.copy('act' if ei % 2 == 0 else 'dve', dst, src, [pk], [('acs', n)])
                ei += 1
        P.barrier()
        yb = fz
        ack = [('acs', n) for n in range(NCH)]
        for kt in range(T // KT):
            tb_, tk = trot.next()
            for cs in range(2):
                self.dma('sp', tb_[:, cs, :, :], st['dft'][cs][:, kt * KT:(kt + 1) * KT].rearrange("(n p) k -> p n k", p=128), [], [tk])
            for g in range(4):
                pt, pk = nps()
                for n in range(NCH):
                    for cs in range(2):
                        self.mm(pt[:, 0:KT], Acs[:, n, g, cs * 128:(cs + 1) * 128], tb_[:, cs, n, :],
                                n == 0 and cs == 0, n == NCH - 1 and cs == 1, ack + [tk], [pk])
                self.copy('act' if ei % 2 == 0 else 'dve', yb[:, g, kt * KT:(kt + 1) * KT], pt[:, 0:KT], [pk], [('yb', g)])
                ei += 1
        for g in range(4):
            self.dma('sp', st['yT'][512 + g * 128:512 + (g + 1) * 128, :], yb[:, g, :], [('yb', g)], ['yT'])
        P.barrier()

    def stage_sgu(self, l, st):
        P = self.P
        T = st['T']; NCH = T // 128
        self.arena_reset()
        vs = self.alloc([128, NCH, 512], BF16)
        vg = self.alloc([128, NCH, 512], BF16)
        uT = self.alloc([128, 4, T], BF16)
        yc = self.alloc([128, 4, T], BF16)
        wsf = self.alloc([128, 4, 128], F32); wsb = self.alloc([128, 4, 128], BF16)
        bsb = self.alloc([128, 4, 512], F32)
        s1 = self.alloc([128, NCH * 4], F32); s2 = self.alloc([128, NCH * 4], F32); mn = self.alloc([128, NCH * 4], F32)
        PCS = 4
        sqrot = self.rot('sq', 2, [128, PCS * 512], F32)
        tmrot = self.rot('tm', 3, [128, 512], F32)
        self.dma('sp', vs, st['vs_tok'].rearrange("(n p) c -> p n c", p=128), [], ['vs'])
        for g in range(4):
            self.dma('sp', uT[:, g, :], st['uT'][g * 128:(g + 1) * 128, :], [], [('u', g)])
        self.dma('sp', wsf, self.W['sgu_wT'][l], [], ['wsf'])
        self.dma('sp', bsb, self.W['sgu_bb'][l], [], ['bsb'])
        self.copy('dve', wsb, wsf, ['wsf'], ['wsb'])
        for n0 in range(0, NCH, PCS):
            g_ = min(PCS, NCH - n0)
            v3 = vs[:, n0:n0 + g_, :].rearrange("p n (g c) -> p (n g) c", g=4)
            self.P.op('dve', lambda e, v3=v3, n0=n0, g_=g_: e.reduce_sum(s1[:, n0 * 4:(n0 + g_) * 4], v3, axis=AX.X),
                      reads=['vs'], writes=[('s1', n0)])
            sq, sk = sqrot.next()
            sqv = sq[:, 0:g_ * 512]
            self.act(sqv, vs[:, n0:n0 + g_, :].rearrange("p n c -> p (n c)"), AF.Square, ['vs'], [sk])
            self.P.op('dve', lambda e, sqv=sqv, n0=n0, g_=g_: e.reduce_sum(s2[:, n0 * 4:(n0 + g_) * 4],
                                                                           sqv.rearrange("p (a c) -> p a c", c=128), axis=AX.X),
                      reads=[sk], writes=[('s2', n0)])
        s1k = [('s1', n0) for n0 in range(0, NCH, PCS)]; s2k = [('s2', n0) for n0 in range(0, NCH, PCS)]
        self.ts('dve', mn, s1, 1.0 / 128, None, ALU.mult, None, s1k, ['mn'])
        self.tt('dve', s1, mn, mn, ALU.mult, ['mn'], ['msq'])
        self.stt('dve', s2, s2, 1.0 / 128, s1, ALU.mult, ALU.subtract, s2k + ['msq'], ['var'])
        self.rstd(s2, s2, 1.0, ['var'], 'rstdv')
        for n in range(NCH):
            for g in range(4):
                i = n * 4 + g
                self.ts('dve', vg[:, n, g * 128:(g + 1) * 128], vs[:, n, g * 128:(g + 1) * 128],
                        mn[:, i:i + 1], s2[:, i:i + 1], ALU.subtract, ALU.mult, ['vs', 'mn', 'rstdv'], [('vg', n)])
        psi = [0]

        def nps():
            k = psi[0] % 8; psi[0] += 1
            return self.ps[k][:], ('ps', k)
        G = 4
        for g in range(4):
            for n0 in range(0, NCH, G):
                g_ = min(G, NCH - n0)
                pt, pk = nps()
                for n in range(n0, n0 + g_):
                    self.mm(pt[:, (n - n0) * 128:(n - n0 + 1) * 128], vg[:, n, g * 128:(g + 1) * 128], wsb[:, g, :], True, True,
                            [('vg', n), 'wsb'], [pk])
                tm, tk = tmrot.next()
                w_ = g_ * 128
                self.tt('dve', tm[:, 0:w_], pt[:, 0:w_], bsb[:, g, 0:w_], ALU.add, [pk, 'bsb'], [tk])
                self.tt('pool', yc[:, g, n0 * 128:n0 * 128 + w_], tm[:, 0:w_], uT[:, g, n0 * 128:n0 * 128 + w_], ALU.mult,
                        [tk, ('u', g)], [('yc', g)])
            self.dma('sp', st['yT'][1024 + g * 128:1024 + (g + 1) * 128, :], yc[:, g, :], [('yc', g)], ['yT'])
        P.barrier()

    def stage_conv(self, l, st):
        P = self.P
        T = st['T']; RW = st['RW']; NR = T // RW
        self.arena_reset()
        brot = self.rot('cb', 2, [128, T], BF16); crot = self.rot('cc', 2, [128, T], BF16); xrot = self.rot('cx', 2, [128, T], BF16)
        yrot = self.rot('cy', 2, [128, T], F32); orot = self.rot('co', 2, [128, T], F32); drot = self.rot('cd', 2, [128, T], BF16)
        for ct in range(4):
            bt, bk = brot.next(); cg, ck = crot.next(); xv, xk = xrot.next()
            self.dma('sp', bt, st['scT'][ct * 128:(ct + 1) * 128, :], [], [bk])
            self.dma('sp', cg, st['scT'][512 + ct * 128:512 + (ct + 1) * 128, :], [], [ck])
            self.dma('sp', xv, st['scT'][1024 + ct * 128:1024 + (ct + 1) * 128, :], [], [xk])
            y, yk = yrot.next(); o, ok = orot.next(); d, dk = drot.next()
            w0 = self.cw[:, (l * 4 + ct) * 3 + 0:(l * 4 + ct) * 3 + 1]
            w1 = self.cw[:, (l * 4 + ct) * 3 + 1:(l * 4 + ct) * 3 + 2]
            w2 = self.cw[:, (l * 4 + ct) * 3 + 2:(l * 4 + ct) * 3 + 3]
            self.tt('dve', y, cg, xv, ALU.mult, [ck, xk], [yk])
            self.ts('pool', o, y, w1, None, ALU.mult, None, [yk, 'cw'], [ok])
            y3 = y.rearrange("p (r w) -> p r w", w=RW); o3 = o.rearrange("p (r w) -> p r w", w=RW)
            self.stt('dve', o3[:, :, 1:RW], y3[:, :, 0:RW - 1], w0, o3[:, :, 1:RW], ALU.mult, ALU.add, [yk, ok, 'cw'], [ok])
            self.stt('dve', o3[:, :, 0:RW - 1], y3[:, :, 1:RW], w2, o3[:, :, 0:RW - 1], ALU.mult, ALU.add, [yk, ok, 'cw'], [ok])
            self.tt('pool', d, o, bt, ALU.mult, [ok, bk], [dk])
            self.dma('sp', st['yT'][1536 + ct * 128:1536 + (ct + 1) * 128, :], d, [dk], ['yT'])
        P.barrier()

    def stage_merge(self, l, st):
        P = self.P
        T = st['T']; TW = min(512, T)
        self.arena_reset()
        wbr = self.alloc([128, 16, D], BF16)
        for n in range(4):
            self.dma('pool', wbr[:, n * 4:(n + 1) * 4, :], self.W['w_branch'][l][n * 512:(n + 1) * 512, :].rearrange("(k p) d -> p k d", p=128),
                     [], [('wbr', n)])
        wbk = [('wbr', n) for n in range(4)]
        yrot = self.rot('yt', 2, [128, 16, TW], BF16)
        grot = self.rot('gt', 3, [128, 4, TW], BF16)
        trot = self.rot('t', 8, [128, TW], F32)
        mrot = self.rot('mg', 2, [128, 16, TW], BF16)
        psi = [0]

        def nps():
            k = psi[0] % 8; psi[0] += 1
            return self.ps[k][:], ('ps', k)
        for t in range(T // TW):
            sl = slice(t * TW, (t + 1) * TW)
            yt, yk = yrot.next()
            self.dma('sp', yt, st['yT'][:, sl].rearrange("(k p) n -> p k n", p=128), ['yT'], [yk])
            mg, mk = mrot.next()
            for dt_ in range(16):
                gt, gk = grot.next()
                self.dma('sp', gt, st['gateT'][:, sl].rearrange("(n r) t -> r n t", n=4)[dt_ * 128:(dt_ + 1) * 128], [], [gk])
                tl = []
                for n in range(4):
                    pt, pk = nps()
                    for mc in range(4):
                        self.mm(pt[:, 0:TW], wbr[:, n * 4 + mc, dt_ * 128:(dt_ + 1) * 128], yt[:, n * 4 + mc, :],
                                mc == 0, mc == 3, wbk + [yk], [pk])
                    tm, tk = trot.next()
                    self.tt('dve', tm, pt[:, 0:TW], gt[:, n, :], ALU.mult, [pk, gk], [tk])
                    tl.append((tm, tk))
                self.tt('pool', tl[0][0], tl[0][0], tl[1][0], ALU.add, [tl[0][1], tl[1][1]], [tl[0][1]])
                self.tt('pool', tl[2][0], tl[2][0], tl[3][0], ALU.add, [tl[2][1], tl[3][1]], [tl[2][1]])
                self.tt('pool', mg[:, dt_, :], tl[0][0], tl[2][0], ALU.add, [tl[0][1], tl[2][1]], [mk])
            self.dma('sp', st['mergedT'][:, sl].rearrange("(k p) n -> p k n", p=128), mg, [mk], ['mergedT'])
        P.barrier()

    def stage_outproj(self, l, st, which):
        P = self.P
        T = st['T']; TW = min(512, T); j = st['j']
        if which == 'mix':
            K = D; wsrc = self.W['w_out'][l]; src = st['mergedT']; G = self.modv[:, 2, :, j]
            res_in = st['res_in']
        else:
            K = DFF; wsrc = self.W['ffn_w_out'][l]; src = st['hT']; G = self.modv[:, 5, :, j]
            res_in = st['res']
        res_out = st['res']
        NKT = K // 128
        self.arena_reset()
        big = (K != D)
        inrot = self.rot('in', 2, [128, NKT, TW], BF16)
        wrot = self.rot('w', 3 if big else 4, [128, NKT, 128], BF16)
        mix = self.alloc([128, 16, TW], F32)
        rrot = self.rot('res', 1 if big else 2, [128, 16, TW], F32)
        sqrot = self.rot('sq', 2 if big else 3, [128, TW], BF16)
        rs = self.alloc([128, TW], F32)
        tmrot = self.rot('tm', 2 if big else 3, [128, TW], F32)
        psi = [0]

        def nps():
            k = 1 + psi[0] % 7; psi[0] += 1
            return self.ps[k][:], ('ps', k)
        nb = T // TW

        def load_in(t):
            sl = slice(t * TW, (t + 1) * TW)
            it, ik = inrot.next()
            self.dma('sp', it, src[:, sl].rearrange("(k p) n -> p k n", p=128), [('src', t)], [ik])
            return it, ik

        def load_res(t):
            sl = slice(t * TW, (t + 1) * TW)
            rt, rk = rrot.next()
            self.dma('sp', rt, res_in[:, sl].rearrange("(k p) n -> p k n", p=128), [('res', t)], [rk])
            return rt, rk
        cur_in = load_in(0); cur_res = load_res(0)
        for t in range(nb):
            sl = slice(t * TW, (t + 1) * TW)
            it, ik = cur_in; rt, rk = cur_res
            ss = self.ps[0][:, 0:TW]
            for dt_ in range(16):
                wt, wk = wrot.next()
                self.dma('pool', wt.rearrange("p k n -> p (k n)"), wsrc[dt_], [], [wk], max_dma_last_dim=8192)
                pt, pk = nps()
                for k in range(NKT):
                    self.mm(pt[:, 0:TW], wt[:, k, :], it[:, k, :], k == 0, k == NKT - 1, [wk, ik], [pk])
                self.copy('dve', mix[:, dt_, :], pt[:, 0:TW], [pk], [('mix', dt_)])
                sq, sk = sqrot.next()
                self.act(sq, mix[:, dt_, :], AF.Square, [('mix', dt_)], [sk])
                self.mm(ss, self.ones, sq, dt_ == 0, dt_ == 15, [sk, 'cm'], ['ps0'])
            if t + 1 < nb:
                cur_in = load_in(t + 1)
                if not big: cur_res = load_res(t + 1)
            self.rstd(rs, ss, 1.0 / D, ['ps0'], 'rs')
            for dt_ in range(16):
                tm, tk = tmrot.next()
                self.stt('dve', tm, mix[:, dt_, :], G[:, dt_:dt_ + 1], rs, ALU.mult, ALU.mult, [('mix', dt_), 'rs', 'modv'], [tk])
                self.tt('pool' if dt_ % 2 == 0 else 'dve', rt[:, dt_, :], rt[:, dt_, :], tm, ALU.add, [rk, tk], [rk])
            self.dma('sp', res_out[:, sl].rearrange("(k p) n -> p k n", p=128), rt, [rk], [('res', t)])
            if t + 1 < nb and big:
                cur_res = load_res(t + 1)
        P.barrier()


def _bf(a):
    return np.asarray(a, dtype=np.float32).astype(ml_dtypes.bfloat16)


def _rope_tables(T, latent):
    cos = np.ones((128, T), np.float32); sin = np.zeros((128, T), np.float32)
    if latent:
        t = np.arange(T)
        n = 64
        freqs = (np.float32(10000.0) ** (-np.arange(0, n, 2, dtype=np.float32) / np.float32(n))).astype(np.float32)
        for half, pos in ((0, t // 64), (1, t % 64)):
            ang = pos.astype(np.float32)[None, :] * freqs[:, None]
            c = np.cos(ang).astype(np.float32); s = np.sin(ang).astype(np.float32)
            b = half * 64
            cos[b:b + 32] = c; cos[b + 32:b + 64] = c
            sin[b:b + 32] = -s; sin[b + 32:b + 64] = s
    return np.stack([cos, sin]).astype(np.float32)


def _dft_tables(T):
    k = np.arange(T, dtype=np.int64)
    ang = 2.0 * np.pi * ((k[:, None] * k[None, :]) % T).astype(np.float64) / T
    sc = 1.0 / np.sqrt(T * 128.0)
    return np.stack([_bf(np.cos(ang) * sc), _bf(-np.sin(ang) * sc)])


def _consts():
    C = 128
    i = np.arange(C, dtype=np.float32)
    rel = i[None, :] - i[:, None]
    s = np.float32(128 ** -0.5)
    dconst = np.zeros((128, 6 * 128 + 4), np.float32)
    dconst[:, 0:128] = np.maximum(rel, 0)
    dconst[:, 128:256] = (rel >= 0).astype(np.float32) * s
    dconst[:, 256:384] = np.maximum(-rel, 0)
    dconst[:, 384:512] = (rel <= 0).astype(np.float32) * s
    dconst[:, 512:640] = (i + 1)[None, :]
    dconst[:, 640:768] = (C - i)[None, :]
    dconst[:, 768] = C - 1 - i
    dconst[:, 769] = i
    dconst[:, 770] = C
    ident = np.eye(128, dtype=np.float32)
    perm = np.zeros((128, 128), np.float32)
    for d in range(128):
        partner = d + 32 if (d % 64) < 32 else d - 32
        perm[partner, d] = 1.0
    cmat = _bf(np.concatenate([ident, perm, np.ones((128, 128), np.float32)], axis=1))
    c = np.arange(128, dtype=np.int64)
    ang = 2.0 * np.pi * ((c[:, None] * c[None, :]) % 128).astype(np.float64) / 128
    dftc = _bf(np.concatenate([np.cos(ang), np.sin(ang)], axis=1))
    return dconst, cmat, dftc


def prepare_inputs(x, c, ctx, c_ctx, ada_w, ada_b, norm_g, w_in, ret_log_decay, conv_w, sgu_w, sgu_b,
                   w_branch, w_out, ffn_w_in, ffn_w_out, cores=range(NCORES), depth=DEPTH):
    f = lambda a: np.ascontiguousarray(np.asarray(a, dtype=np.float32))
    x = np.asarray(x); ctx = np.asarray(ctx); c = np.asarray(c); c_ctx = np.asarray(c_ctx)
    dconst, cmat, dftc = _consts()
    shared = {
        "ada_w": f(np.asarray(ada_w)[:depth]),
        "ada_b": f(np.asarray(ada_b).reshape(DEPTH, 96, 128).transpose(0, 2, 1)[:depth]),
        "norm_g": f(np.asarray(norm_g).reshape(DEPTH, 4, 16, 128).transpose(3, 0, 1, 2).reshape(128, -1)),
        "w_in": f(np.asarray(w_in)[:depth]),
        "lgb": f(np.broadcast_to(np.asarray(ret_log_decay).reshape(1, DEPTH * 8), (128, DEPTH * 8))),
        "conv_w": f(np.asarray(conv_w).reshape(DEPTH, 3, 4, 128).transpose(3, 0, 2, 1).reshape(128, -1)),
        "sgu_wT": f(np.asarray(sgu_w).transpose(0, 3, 1, 2)[:depth]),
        "sgu_bb": f(np.broadcast_to(np.asarray(sgu_b)[:, None, :, None, :], (DEPTH, 128, 4, 4, 128)).reshape(DEPTH, 128, 4, 512)[:depth]),
        "w_branch": f(np.asarray(w_branch).reshape(DEPTH, 4 * MIXW, D)[:depth]),
        "w_out": f(np.asarray(w_out)[:depth].reshape(depth, 16, 128, 16, 128).transpose(0, 3, 2, 1, 4).reshape(depth, 16, 128, D)),
        "ffn_w_in": f(np.asarray(ffn_w_in)[:depth]),
        "ffn_w_out": f(np.asarray(ffn_w_out)[:depth].reshape(depth, 44, 128, 16, 128).transpose(0, 3, 2, 1, 4).reshape(depth, 16, 128, DFF)),
        "dconst": dconst, "cmat": cmat, "dftc": dftc,
        "rope_l": _rope_tables(SEQ, True), "rope_c": _rope_tables(CTX, False),
        "dft_l": _dft_tables(SEQ), "dft_cx": _dft_tables(CTX),
    }
    in_maps = []
    for b in cores:
        m = dict(shared)
        m["xT"] = f(x[b].T)
        m["ctxT"] = f(ctx[b].T)
        cc2 = np.stack([c[b], c_ctx], axis=1).astype(np.float32)
        m["cc"] = f(cc2.reshape(16, 128, 2).transpose(1, 0, 2))
        in_maps.append(m)
    return in_maps


_NC_CACHE = {}


def kernel(**inputs):
    if 'nc' not in _NC_CACHE:
        _NC_CACHE['nc'] = Builder().build()
    nc = _NC_CACHE['nc']
    in_maps = prepare_inputs(**inputs)
    res = run_bass_kernel_spmd(nc, in_maps, core_ids=list(range(NCORES)))
    out = np.stack([np.asarray(r["outT"]).T for r in res.results], axis=0)
    return np.ascontiguousarray(out.astype(np.float32))
```

```python
import numpy as np
import ml_dtypes
import concourse.bass as bass
import concourse.mybir as mybir
from concourse.bass_utils import run_bass_kernel_spmd

F32 = mybir.dt.float32
BF16 = mybir.dt.bfloat16
U8 = mybir.dt.uint8
AF = mybir.ActivationFunctionType
ALU = mybir.AluOpType
AX = mybir.AxisListType

D = 2048
SEQ = 4096
CTX = 256
DEPTH = 4
MIXW = 512
DFF = 5632
INW = 13312
EPS = 1e-6
NCORES = 8

ENGS = ['pe', 'act', 'dve', 'pool', 'sp']
EIDX = {e: i for i, e in enumerate(ENGS)}
NDMA_SEM = 10
DMA_ENGS = ['sp', 'pool', 'act']
NK = len(ENGS) + len(DMA_ENGS) * NDMA_SEM


class Op:
    __slots__ = ('eng', 'fn', 'reads', 'writes', 'dma', 'deps', 'sig', 'sem', 'val', 'idx', 'clock', 'kidx')

    def __init__(s, eng, fn, reads, writes, dma):
        s.eng = eng; s.fn = fn; s.reads = reads; s.writes = writes; s.dma = dma
        s.deps = []; s.sig = False; s.sem = None; s.val = 0; s.clock = None; s.kidx = EIDX[eng]


class Prog:
    def __init__(self, nc):
        self.nc = nc
        self.ops = []
        self.lastw = {}
        self.readers = {}
        self.last_on = {e: None for e in ENGS}
        self.bar_deps = []
        self.bar_pending = set()
        self.dma_cnt = {e: 0 for e in DMA_ENGS}
        self.dma_last = {}

    def op(self, eng, fn, reads=(), writes=(), dma=False):
        pr_ = [r for r in reads if r == 'ps0' or (isinstance(r, tuple) and r[0] == 'ps')]
        if pr_:
            writes = tuple(writes) + tuple(r for r in pr_ if r not in writes)
        o = Op(eng, fn, tuple(reads), tuple(writes), dma)
        o.idx = len(self.ops)
        deps = set()
        if eng in self.bar_pending:
            deps.update(self.bar_deps); self.bar_pending.discard(eng)
        lastw = self.lastw
        for r in o.reads:
            w = lastw.get(r)
            if w is not None: deps.add(w)
        for r in o.writes:
            w = lastw.get(r)
            if w is not None: deps.add(w)
            rs = self.readers.get(r)
            if rs: deps.update(rs.values())
        if dma:
            k = self.dma_cnt[eng]; self.dma_cnt[eng] += 1
            slot = k % NDMA_SEM
            o.kidx = len(ENGS) + DMA_ENGS.index(eng) * NDMA_SEM + slot
            o.val = 16 * (k // NDMA_SEM + 1)
            prev = self.dma_last.get(o.kidx)
            if prev is not None: deps.add(prev)
            self.dma_last[o.kidx] = o.idx
            o.sig = True
        ops = self.ops
        for d in deps:
            p = ops[d]
            if p.eng == eng and not p.dma and not dma:
                if eng == 'pe': continue
            o.deps.append(d)
        rk = ('d', o.idx) if dma else eng
        for r in o.reads:
            self.readers.setdefault(r, {})[rk] = o.idx
        for r in o.writes:
            lastw[r] = o.idx
            self.readers[r] = {}
        ops.append(o)
        self.last_on[eng] = o.idx
        return o

    def barrier(self):
        deps = set()
        for e in ENGS:
            if self.last_on[e] is not None: deps.add(self.last_on[e])
        deps.update(self.dma_last.values())
        self.bar_deps = sorted(deps)
        self.bar_pending = set(ENGS)
        self.lastw = {}; self.readers = {}

    def finish(self):
        self.barrier()
        self.op('sp', lambda e: e.nop())

    def emit(self, new_sem):
        ops = self.ops
        for o in ops:
            for d in o.deps: ops[d].sig = True
        sems = [new_sem(f"k{i}") for i in range(NK)]
        cnt = [0] * len(ENGS)
        for o in ops:
            if not o.dma:
                if o.sig: cnt[o.kidx] += 1
                o.val = cnt[o.kidx]
            o.sem = sems[o.kidx]
        self.sig_counts = cnt
        know = {e: [0] * NK for e in ENGS}
        streams = {e: [] for e in ENGS}
        for o in ops:
            kc = know[o.eng]
            waits = {}
            for d in o.deps:
                p = ops[d]
                if kc[p.kidx] >= p.val: continue
                if waits.get(p.kidx, 0) < p.val: waits[p.kidx] = p.val
            for d in o.deps:
                p = ops[d]
                if kc[p.kidx] >= p.val: continue
                pc = p.clock
                if pc is not None:
                    for i in range(NK):
                        if pc[i] > kc[i]: kc[i] = pc[i]
                if kc[p.kidx] < p.val: kc[p.kidx] = p.val
            if o.sig:
                c2 = list(kc)
                if c2[o.kidx] < o.val: c2[o.kidx] = o.val
                o.clock = c2
            streams[o.eng].append((o, [(sems[k], v) for k, v in waits.items()]))
        return streams


def run_streams(nc, streams):
    engmap = {'pe': 'tensor', 'act': 'scalar', 'dve': 'vector', 'pool': 'gpsimd', 'sp': 'sync'}
    with nc.Block() as block:
        for e in ENGS:
            lst = streams[e]

            def body(eng, lst=lst):
                for o, waits in lst:
                    for s, v in waits:
                        eng.wait_ge(s, v)
                    ins = o.fn(eng)
                    if o.sig:
                        ins.then_inc(o.sem, 16 if o.dma else 1)
            getattr(block, engmap[e])(body)


ARENA_BYTES = 196 * 1024


class Rot:
    def __init__(self, name, aps):
        self.name = name; self.aps = aps; self.i = 0

    def next(self):
        k = self.i % len(self.aps); self.i += 1
        return self.aps[k], (self.name, k)


class Builder:
    def __init__(self, depth=DEPTH, taps=(), stop_after=None):
        self.depth = depth
        self.taps = set(taps)
        self.stop_after = stop_after
        nc = self.nc = bass.Bass("TRN2", target_bir_lowering=False)
        self.P = Prog(nc)
        self.din = {}
        self._uid = 0

    def inp(self, name, shape, dt):
        t = self.nc.dram_tensor(name, list(shape), dt, kind="ExternalInput").ap()
        self.din[name] = t
        return t

    def scratch(self, name, shape, dt):
        kind = "ExternalOutput" if name in self.taps else "Internal"
        return self.nc.dram_tensor(name, list(shape), dt, kind=kind).ap()

    def arena_reset(self):
        self.aoff = 0

    def alloc(self, shape, dt):
        esz = 4 if dt == F32 else 2
        n = int(np.prod(shape[1:])) * esz
        n_al = (n + 63) // 64 * 64
        assert self.aoff + n_al <= ARENA_BYTES, (self.aoff, n_al)
        ap = self.arena[:, self.aoff:self.aoff + n].bitcast(dt)
        self.aoff += n_al
        if len(shape) == 3:
            ap = ap.rearrange("p (a b) -> p a b", a=shape[1])
        elif len(shape) == 4:
            ap = ap.rearrange("p (a b c) -> p a b c", a=shape[1], b=shape[2])
        return ap

    def rot(self, name, n, shape, dt):
        return Rot(name, [self.alloc(shape, dt) for _ in range(n)])

    def uid(self, s):
        self._uid += 1
        return f"{s}{self._uid}"

    def dma(self, q, out, in_, reads, writes, **kw):
        self.P.op(q, lambda e: e.dma_start(out=out, in_=in_, **kw), reads=reads, writes=writes, dma=True)

    def mm(self, out, lhsT, rhs, start, stop, reads, writes):
        self.P.op('pe', lambda e: e.matmul(out, lhsT, rhs, start=start, stop=stop), reads=reads, writes=writes)

    def act(self, out, in_, func, reads, writes, scale=1.0, bias=None, accum_out=None):
        kw = {}
        if bias is not None: kw['bias'] = bias
        if accum_out is not None: kw['accum_out'] = accum_out
        self.P.op('act', lambda e: e.activation(out, in_, func, scale=scale, **kw), reads=reads, writes=writes)

    def tt(self, eng, out, in0, in1, op, reads, writes):
        self.P.op(eng, lambda e: e.tensor_tensor(out, in0, in1, op), reads=reads, writes=writes)

    def ts(self, eng, out, in0, s1, s2, op0, op1, reads, writes):
        if op1 is None:
            self.P.op(eng, lambda e: e.tensor_scalar(out, in0, s1, None, op0), reads=reads, writes=writes)
        else:
            self.P.op(eng, lambda e: e.tensor_scalar(out, in0, s1, s2, op0, op1), reads=reads, writes=writes)

    def stt(self, eng, out, in0, scalar, in1, op0, op1, reads, writes):
        self.P.op(eng, lambda e: e.scalar_tensor_tensor(out, in0, scalar, in1, op0, op1), reads=reads, writes=writes)

    def copy(self, eng, out, in_, reads, writes):
        if eng == 'act':
            self.act(out, in_, AF.Copy, reads, writes)
        else:
            self.P.op(eng, lambda e: e.tensor_copy(out, in_), reads=reads, writes=writes)

    def rstd(self, out, ss, inv_n, reads, key):
        self.ts('dve', out, ss, inv_n, EPS, ALU.mult, ALU.add, reads, [key])
        self.act(out, out, AF.Sqrt, [key], [key])
        self.P.op('dve', lambda e: e.reciprocal(out, out), reads=[key], writes=[key])

    def build(self):
        nc = self.nc
        depth = self.depth
        xT = self.inp("xT", [D, SEQ], F32)
        ctxT = self.inp("ctxT", [D, CTX], F32)
        cc = self.inp("cc", [128, 16, 2], F32)
        DEPTH = self.depth
        ada_w = self.inp("ada_w", [DEPTH, D, 6 * D], F32)
        ada_b = self.inp("ada_b", [DEPTH, 128, 96], F32)
        norm_g = self.inp("norm_g", [128, 4 * 4 * 16], F32)
        w_in = self.inp("w_in", [DEPTH, D, INW], F32)
        lgb = self.inp("lgb", [128, 4 * 8], F32)
        conv_w = self.inp("conv_w", [128, 4 * 4 * 3], F32)
        sgu_wT = self.inp("sgu_wT", [DEPTH, 128, 4, 128], F32)
        sgu_bb = self.inp("sgu_bb", [DEPTH, 128, 4, 512], F32)
        w_branch = self.inp("w_branch", [DEPTH, 4 * MIXW, D], F32)
        w_out = self.inp("w_out", [DEPTH, 16, 128, D], F32)
        ffn_w_in = self.inp("ffn_w_in", [DEPTH, D, 2 * DFF], F32)
        ffn_w_out = self.inp("ffn_w_out", [DEPTH, 16, 128, DFF], F32)
        dconst = self.inp("dconst", [128, 6 * 128 + 4], F32)
        cmat = self.inp("cmat", [128, 3 * 128], BF16)
        dftc = self.inp("dftc", [128, 256], BF16)
        rope_l = self.inp("rope_l", [2, 128, SEQ], F32)
        rope_c = self.inp("rope_c", [2, 128, CTX], F32)
        dft_l = self.inp("dft_l", [2, SEQ, SEQ], BF16)
        dft_cx = self.inp("dft_cx", [2, CTX, CTX], BF16)
        outT = nc.dram_tensor("outT", [D, SEQ], F32, kind="ExternalOutput").ap()

        def mkstream(nm, T):
            s = dict(name=nm, T=T)
            s['qkT'] = self.scratch(f"{nm}_qkT", [1024, T], BF16)
            s['v_tok'] = self.scratch(f"{nm}_vtok", [T, 512], BF16)
            s['sgT'] = self.scratch(f"{nm}_sgT", [512, T], BF16)
            s['fT'] = self.scratch(f"{nm}_fT", [512, T], BF16)
            s['uT'] = self.scratch(f"{nm}_uT", [512, T], BF16)
            s['vs_tok'] = self.scratch(f"{nm}_vstok", [T, 512], BF16)
            s['scT'] = self.scratch(f"{nm}_scT", [1536, T], BF16)
            s['gateT'] = self.scratch(f"{nm}_gateT", [8192, T], BF16)
            s['yT'] = self.scratch(f"{nm}_yT", [2048, T], BF16)
            s['mergedT'] = self.scratch(f"{nm}_mergedT", [2048, T], BF16)
            s['hT'] = self.scratch(f"{nm}_hT", [DFF, T], BF16)
            return s
        lat = mkstream("lat", SEQ)
        lat.update(j=0, RW=64, rope=rope_l, dft=dft_l, TB=2048, res_in=xT, res=outT)
        ctx = mkstream("ctx", CTX)
        ctxres = self.scratch("ctx_res", [D, CTX], F32)
        ctx.update(j=1, RW=CTX, rope=rope_c, dft=dft_cx, TB=CTX, res_in=ctxT, res=ctxres)
        self.s0 = self.scratch("s0", [8, 128, 128], F32)

        self.modv = nc.alloc_sbuf_tensor("modv", [128, 6, 16, 2], F32)
        self.ng = nc.alloc_sbuf_tensor("ng", [128, 4 * 4 * 16], F32)
        self.lg = nc.alloc_sbuf_tensor("lg", [128, 4 * 8], F32)
        self.cw = nc.alloc_sbuf_tensor("cw", [128, 4 * 4 * 3], F32)
        self.dc = nc.alloc_sbuf_tensor("dc", [128, 6 * 128 + 4], F32)
        self.cm = nc.alloc_sbuf_tensor("cm", [128, 3 * 128], BF16)
        self.scc = nc.alloc_sbuf_tensor("scc", [128, 16, 2], BF16)
        self.ccf = nc.alloc_sbuf_tensor("ccf", [128, 16, 2], F32)
        self.arena = nc.alloc_sbuf_tensor("arena", [128, ARENA_BYTES], U8)
        self.ps = [nc.alloc_psum_tensor(f"ps{i}", [128, 512], F32) for i in range(8)]
        self.ident = self.cm[:, 0:128]
        self.perm = self.cm[:, 128:256]
        self.ones = self.cm[:, 256:384]

        P = self.P
        for (dst, src, nm) in ((self.ng, norm_g, 'ng'), (self.lg, lgb, 'lg'), (self.cw, conv_w, 'cw'),
                               (self.dc, dconst, 'dc'), (self.cm, cmat, 'cm')):
            self.dma('sp', dst[:], src[:], [], [nm])
        self.dma('sp', self.ccf[:], cc[:], [], ['ccf'])
        self.act(self.scc[:], self.ccf[:], AF.Silu, ['ccf'], ['scc'])
        P.barrier()

        W = dict(ada_w=ada_w, ada_b=ada_b, w_in=w_in, sgu_wT=sgu_wT, sgu_bb=sgu_bb, w_branch=w_branch,
                 w_out=w_out, ffn_w_in=ffn_w_in, ffn_w_out=ffn_w_out, dftc=dftc)
        self.W = W
        done = False
        for l in range(depth):
            last = (l == 3)
            self.stage_adaln(l)
            for st in (ctx, lat):
                def stop(nm):
                    return self.stop_after == st['name'] + '_' + nm
                if self.stop_after == 'adaln': done = True; break
                self.stage_gemm_in(l, st, which='inproj')
                if stop('inproj'): done = True; break
                self.stage_retention(l, st)
                if stop('ret'): done = True; break
                if st is ctx and last:
                    continue
                self.stage_fourier(l, st)
                if stop('fourier'): done = True; break
                self.stage_sgu(l, st)
                if stop('sgu'): done = True; break
                self.stage_conv(l, st)
                if stop('conv'): done = True; break
                self.stage_merge(l, st)
                if stop('merge'): done = True; break
                self.stage_outproj(l, st, which='mix')
                if stop('sub1'): done = True; break
                self.stage_gemm_in(l, st, which='ffn1')
                if stop('ffn1'): done = True; break
                self.stage_outproj(l, st, which='ffn2')
                if stop('ffn2'): done = True; break
                st['res_in'] = st['res']
            if done: break
        P.finish()
        streams = P.emit(lambda name: nc.alloc_semaphore(name))
        run_streams(nc, streams)
        return nc

    def stage_adaln(self, l):
        P = self.P
        self.arena_reset()
        wrot = self.rot('aw', 3, [128, 16, 512], BF16)
        ab = self.alloc([128, 96], F32)
        mod = self.alloc([128, 96, 2], F32)
        self.dma('sp', ab, self.W['ada_b'][l], [], ['ab'])
        pst = self.ps[0][:]
        psv = pst[:, 0:192].rearrange("p (a b) -> p a b", b=2)
        for c in range(24):
            wt, wk = wrot.next()
            self.dma('pool', wt, self.W['ada_w'][l][:, c * 512:(c + 1) * 512].rearrange("(k p) n -> p k n", p=128), [], [wk])
            for m in range(4):
                ct = c * 4 + m
                for k in range(16):
                    self.mm(pst[:, ct * 2:ct * 2 + 2], wt[:, k, m * 128:(m + 1) * 128], self.scc[:, k, :],
                            k == 0, k == 15, [wk, 'scc'], ['ps0'])
        for j in range(2):
            self.tt('dve', mod[:, :, j], psv[:, :, j], ab, ALU.add, ['ps0', 'ab'], ['mod'])
        ngl = self.ng[:, l * 64:(l + 1) * 64].rearrange("p (i k) -> p i k", i=4)
        mv = self.modv
        for j in range(2):
            self.stt('dve', mv[:, 0, :, j], mod[:, 16:32, j], 1.0, ngl[:, 0, :], ALU.add, ALU.mult, ['mod', 'ng'], ['modv'])
            self.copy('dve', mv[:, 1, :, j], mod[:, 0:16, j], ['mod'], ['modv'])
            self.tt('dve', mv[:, 2, :, j], mod[:, 32:48, j], ngl[:, 1, :], ALU.mult, ['mod', 'ng'], ['modv'])
            self.stt('dve', mv[:, 3, :, j], mod[:, 64:80, j], 1.0, ngl[:, 2, :], ALU.add, ALU.mult, ['mod', 'ng'], ['modv'])
            self.copy('dve', mv[:, 4, :, j], mod[:, 48:64, j], ['mod'], ['modv'])
            self.tt('dve', mv[:, 5, :, j], mod[:, 80:96, j], ngl[:, 3, :], ALU.mult, ['mod', 'ng'], ['modv'])
        if 'tap_modv' in self.taps:
            tap = self.scratch('tap_modv', [128, 192], F32)
            self.dma('sp', tap, mv[:].rearrange("p a b c -> p (a b c)"), ['modv'], ['tapm'])
        P.barrier()

    def stage_gemm_in(self, l, st, which):
        P = self.P
        T = st['T']; TB = st['TB']; j = st['j']
        TT = 256
        mi = 0 if which == 'inproj' else 3
        A = self.modv[:, mi, :, j]; S = self.modv[:, mi + 1, :, j]
        src = st['res_in'] if which == 'inproj' else st['res']
        for tb in range(T // TB):
            self.arena_reset()
            hx = self.alloc([128, 16, TB], BF16)
            xrot = self.rot('xin', 2, [128, 16, TT], F32)
            sqrot = self.rot('sq', 4, [128, TT], BF16)
            rsrot = self.rot('rs', 2, [128, TT], F32)
            tmrot = self.rot('tm', 3, [128, 512], F32)
            nw = 3 if which == 'inproj' else 4
            wrot = self.rot('w', nw, [128, 16, 512], BF16)
            orot = self.rot('ob', 3, [128, TB], BF16)
            otrot = self.rot('obt', 3, [128, 512], BF16)
            psi = [0]

            def nps():
                k = 1 + psi[0] % 7; psi[0] += 1
                return self.ps[k][:], ('ps', k)
            for t in range(TB // TT):
                t0 = tb * TB + t * TT
                xt, xk = xrot.next()
                self.dma('sp', xt, src[:, t0:t0 + TT].rearrange("(k p) n -> p k n", p=128), ['res'], [xk])
                ss = self.ps[0][:, 0:TT]
                for k in range(16):
                    sq, sk = sqrot.next()
                    self.act(sq, xt[:, k, :], AF.Square, [xk], [sk])
                    self.mm(ss, self.ones, sq, k == 0, k == 15, [sk], ['ps0'])
                rs, rk = rsrot.next()
                self.rstd(rs, ss, 1.0 / D, ['ps0'], rk)
                for k in range(16):
                    tm, tk = tmrot.next()
                    tmv = tm[:, 0:TT]
                    self.stt('dve', tmv, xt[:, k, :], A[:, k:k + 1], rs, ALU.mult, ALU.mult, [xk, rk, 'modv'], [tk])
                    self.act(hx[:, k, t * TT:(t + 1) * TT], tmv, AF.Identity, [tk, 'modv'], [('hx', t)], bias=S[:, k:k + 1])
            hxr = [('hx', t) for t in range(TB // TT)]
            import os as _os
            if _os.environ.get('KDBG_SKIP_GEMM'):
                P.barrier(); continue
            if which == 'inproj':
                wsrc = self.W['w_in'][l]
                plan = []
                plan.append((0, 'B', AF.Copy, st['qkT'], 0))
                plan.append((1, 'B', AF.Copy, st['qkT'], 512))
                plan.append((2, 'A', AF.Copy, st['v_tok'], 0))
                plan.append((4, 'B', AF.Copy, st['fT'], 0))
                for c in (7, 8, 9):
                    plan.append((c, 'B', AF.Copy, st['scT'], (c - 7) * 512))
                plan.append((3, 'B', AF.Silu, st['sgT'], 0))
                plan.append((5, 'B', AF.Gelu_apprx_tanh, st['uT'], 0))
                plan.append((6, 'A', AF.Gelu_apprx_tanh, st['vs_tok'], 0))
                for c in range(10, 26):
                    plan.append((c, 'B', AF.Sigmoid, st['gateT'], (c - 10) * 512))
                ei = 0
                if _os.environ.get('KDBG_COPYONLY'):
                    plan = [(c, ori, AF.Copy, dst, off) for (c, ori, func, dst, off) in plan]
                if _os.environ.get('KDBG_PLAN'):
                    plan = [plan[int(i_)] for i_ in _os.environ['KDBG_PLAN'].split(',')]
                if _os.environ.get('KDBG_NOA'):
                    plan = [p_ for p_ in plan if p_[1] == 'B']
                if _os.environ.get('KDBG_ONLYA'):
                    plan = [p_ for p_ in plan if p_[1] == 'A']
                for (c, ori, func, dst, off) in plan:
                    wt, wk = wrot.next()
                    self.dma('pool', wt, wsrc[:, c * 512:(c + 1) * 512].rearrange("(k p) n -> p k n", p=128), [], [wk])
                    if ori == 'B':
                        for m in range(4):
                            ob, ok = orot.next()
                            for t in range(TB // 512 if TB >= 512 else 1):
                                tw = min(512, TB)
                                pt, pk = nps()
                                for k in range(16):
                                    self.mm(pt[:, 0:tw], wt[:, k, m * 128:(m + 1) * 128], hx[:, k, t * tw:(t + 1) * tw],
                                            k == 0, k == 15, [wk] + hxr, [pk])
                                if func == AF.Copy and ei % 2 == 1:
                                    self.copy('dve', ob[:, t * tw:(t + 1) * tw], pt[:, 0:tw], [pk], [ok])
                                else:
                                    self.act(ob[:, t * tw:(t + 1) * tw], pt[:, 0:tw], func, [pk], [ok])
                                ei += 1
                            r0 = off + m * 128
                            self.dma('sp', dst[r0:r0 + 128, tb * TB:(tb + 1) * TB], ob, [ok], [self.uid('st')])
                    else:
                        for tt_ in range(TB // 128):
                            pt, pk = nps()
                            for k in range(16):
                                self.mm(pt, hx[:, k, tt_ * 128:(tt_ + 1) * 128], wt[:, k, :], k == 0, k == 15, [wk] + hxr, [pk])
                            ob, ok = otrot.next()
                            if func == AF.Copy and ei % 2 == 1:
                                self.copy('dve', ob, pt, [pk], [ok])
                            else:
                                self.act(ob, pt, func, [pk], [ok])
                            ei += 1
                            r0 = tb * TB + tt_ * 128
                            self.dma('sp', dst[r0:r0 + 128, :], ob, [ok], [self.uid('st')])
            else:
                wsrc = self.W['ffn_w_in'][l]
                for c in range(DFF // 512):
                    wa, wak = wrot.next()
                    self.dma('pool', wa, wsrc[:, c * 512:(c + 1) * 512].rearrange("(k p) n -> p k n", p=128), [], [wak])
                    wb, wbk = wrot.next()
                    self.dma('pool', wb, wsrc[:, DFF + c * 512:DFF + (c + 1) * 512].rearrange("(k p) n -> p k n", p=128), [], [wbk])
                    for m in range(4):
                        ob, ok = orot.next()
                        for t in range(TB // 512 if TB >= 512 else 1):
                            tw = min(512, TB)
                            pa, pak = nps()
                            for k in range(16):
                                self.mm(pa[:, 0:tw], wa[:, k, m * 128:(m + 1) * 128], hx[:, k, t * tw:(t + 1) * tw],
                                        k == 0, k == 15, [wak] + hxr, [pak])
                            pb, pbk = nps()
                            for k in range(16):
                                self.mm(pb[:, 0:tw], wb[:, k, m * 128:(m + 1) * 128], hx[:, k, t * tw:(t + 1) * tw],
                                        k == 0, k == 15, [wbk] + hxr, [pbk])
                            tm, tk = tmrot.next()
                            self.act(tm[:, 0:tw], pa[:, 0:tw], AF.Silu, [pak], [tk])
                            self.tt('dve', ob[:, t * tw:(t + 1) * tw], tm[:, 0:tw], pb[:, 0:tw], ALU.mult, [tk, pbk], [ok])
                        r0 = c * 512 + m * 128
                        self.dma('sp', st['hT'][r0:r0 + 128, tb * TB:(tb + 1) * TB], ob, [ok], [self.uid('st')])
            P.barrier()

    def stage_retention(self, l, st):
        P = self.P
        T = st['T']; NCH = T // 128
        is_ctx = st['name'] == 'ctx'
        self.arena_reset()
        dc = self.dc
        RELP = dc[:, 0:128]; MPs = dc[:, 128:256]; RELN = dc[:, 256:384]; MNs = dc[:, 384:512]
        IR1 = dc[:, 512:640]; CI = dc[:, 640:768]
        CJ1 = dc[:, 768:769]; JJ = dc[:, 769:770]; CC = dc[:, 770:771]
        SC = float(128 ** -0.5)
        cosT = self.alloc([128, T], F32); sinT = self.alloc([128, T], F32)
        self.dma('sp', cosT, st['rope'][0], [], ['cos'])
        self.dma('sp', sinT, st['rope'][1], [], ['sin'])
        mask = self.alloc([128, 4, 128], F32); mtmp = self.alloc([128, 128], F32)
        QWf = self.alloc([128, 4, 128], F32); QWb = self.alloc([128, 4, 128], F32)
        kw = self.alloc([128, 16], F32)
        for h in range(4):
            lgf = self.lg[:, l * 8 + h:l * 8 + h + 1]; lgbk = self.lg[:, l * 8 + 4 + h:l * 8 + 4 + h + 1]
            mk = ('mask', h)
            self.act(mask[:, h, :], RELP, AF.Exp, ['dc', 'lg'], [mk], scale=lgf)
            self.tt('dve', mask[:, h, :], mask[:, h, :], MPs, ALU.mult, [mk, 'dc'], [mk])
            self.act(mtmp, RELN, AF.Exp, ['dc', 'lg'], ['mtmp'], scale=lgbk)
            self.tt('dve', mtmp, mtmp, MNs, ALU.mult, ['mtmp', 'dc'], ['mtmp'])
            self.tt('dve', mask[:, h, :], mask[:, h, :], mtmp, ALU.add, [mk, 'mtmp'], [mk])
            self.act(QWf[:, h, :], IR1, AF.Exp, ['dc', 'lg'], [('qwf', h)], scale=lgf)
            self.ts('dve', QWf[:, h, :], QWf[:, h, :], SC, None, ALU.mult, None, [('qwf', h)], [('qwf', h)])
            self.act(QWb[:, h, :], CI, AF.Exp, ['dc', 'lg'], [('qwb', h)], scale=lgbk)
            self.ts('dve', QWb[:, h, :], QWb[:, h, :], SC, None, ALU.mult, None, [('qwb', h)], [('qwb', h)])
            self.act(kw[:, h * 4 + 0:h * 4 + 1], CJ1, AF.Exp, ['dc', 'lg'], [('kw', h)], scale=lgf)
            self.act(kw[:, h * 4 + 1:h * 4 + 2], JJ, AF.Exp, ['dc', 'lg'], [('kw', h)], scale=lgbk)
            self.act(kw[:, h * 4 + 2:h * 4 + 3], CC, AF.Exp, ['dc', 'lg'], [('kw', h)], scale=lgf)
            self.act(kw[:, h * 4 + 3:h * 4 + 4], CC, AF.Exp, ['dc', 'lg'], [('kw', h)], scale=lgbk)
        qraw = self.alloc([128, T], BF16); kraw = self.alloc([128, T], BF16)
        qr = self.alloc([128, T], BF16); kr = self.alloc([128, T], BF16)
        qf = self.alloc([128, NCH, 128], BF16); qb = self.alloc([128, NCH, 128], BF16)
        kft = self.alloc([128, NCH, 128], BF16); kbt = self.alloc([128, NCH, 128], BF16)
        vt = self.alloc([128, NCH, 128], BF16)
        sg = self.alloc([128, T], BF16)
        ya = self.alloc([128, T], BF16)
        Sb_all = self.alloc([128, NCH, 128], BF16)
        Sf = self.alloc([128, 128], F32); Sb = self.alloc([128, 128], F32)
        Sfb = self.rot('sfb', 2, [128, 128], BF16)
        ret_all = self.alloc([128, NCH, 128], F32)
        ssq = self.alloc([128, NCH], F32)
        sq_all = self.alloc([128, NCH, 128], F32)
        smrot = self.rot('sm', 3, [128, 128], BF16)
        rnrot = self.rot('rn', 3, [128, 128], BF16)
        t1rot = self.rot('t1', 2, [128, 512], F32)
        t2rot = self.rot('t2', 2, [128, 512], F32)
        psi = [0]

        def nps():
            k = psi[0] % 8; psi[0] += 1
            return self.ps[k][:], ('ps', k)
        import os as _os
        RS = int(_os.environ.get('KDBG_RET', '99'))
        for h in range(4):
            if RS < 2: break
            hk = lambda s: (s, h)
            self.dma('sp', qraw, st['qkT'][h * 128:(h + 1) * 128, :], [], ['qraw'])
            self.dma('sp', kraw, st['qkT'][512 + h * 128:512 + (h + 1) * 128, :], [], ['kraw'])
            self.dma('sp', vt, st['v_tok'][:, h * 128:(h + 1) * 128].rearrange("(n p) e -> p n e", p=128), [], ['vt'])
            self.dma('sp', sg, st['sgT'][h * 128:(h + 1) * 128, :], [], ['sg'])
            if is_ctx:
                self.P.op('dve', lambda e: e.memset(Sf, 0.0), writes=['Sf'])
                self.P.op('dve', lambda e: e.memset(Sb, 0.0), writes=['Sb'])
            else:
                self.dma('sp', Sf, self.s0[h], ['s0'], ['Sf'])
                self.dma('sp', Sb, self.s0[4 + h], ['s0'], ['Sb'])
            TW = min(512, T)
            for (raw, rk, dst, dk) in ((qraw, 'qraw', qr, 'qr'), (kraw, 'kraw', kr, 'kr')):
                for t in range(T // TW):
                    sl = slice(t * TW, (t + 1) * TW)
                    pt, pk = nps()
                    self.mm(pt[:, 0:TW], self.perm, raw[:, sl], True, True, [rk, 'cm'], [pk])
                    t1, t1k = t1rot.next(); t2, t2k = t2rot.next()
                    self.tt('pool', t1[:, 0:TW], raw[:, sl], cosT[:, sl], ALU.mult, [rk, 'cos'], [t1k])
                    self.tt('dve', t2[:, 0:TW], pt[:, 0:TW], sinT[:, sl], ALU.mult, [pk, 'sin'], [t2k])
                    self.tt('dve', dst[:, sl], t1[:, 0:TW], t2[:, 0:TW], ALU.add, [t1k, t2k], [(dk, t)])
            qrk = [('qr', t) for t in range(T // TW)]
            krk = [('kr', t) for t in range(T // TW)]
            if RS < 3: continue
            G = 4
            for n0 in range(0, NCH, G):
                g = min(G, NCH - n0)
                for (dst, QW, nm) in ((qf, QWf, 'qwf'), (qb, QWb, 'qwb')):
                    for n in range(n0, n0 + g):
                        self.tt('pool', dst[:, n, :], qr[:, n * 128:(n + 1) * 128], QW[:, h, :], ALU.mult,
                                qrk + [(nm, h)], [(nm + 'd', n)])
            if RS < 4: continue
            for n in range(NCH):
                pt, pk = nps()
                ptb = pt[:].bitcast(BF16)[:, 0:128]
                self.P.op('pe', lambda e, ptb=ptb, n=n: e.transpose(ptb, kr[:, n * 128:(n + 1) * 128], self.ident),
                          reads=krk + ['cm'], writes=[pk])
                self.act(kft[:, n, :], ptb, AF.Copy, [pk, ('kw', h)], [('kft', n)], scale=kw[:, h * 4:h * 4 + 1])
                self.ts('dve', kbt[:, n, :], ptb, kw[:, h * 4 + 1:h * 4 + 2], None, ALU.mult, None, [pk, ('kw', h)], [('kbt', n)])
            if RS < 5: continue
            for n in range(NCH - 1, -1, -1):
                self.copy('act', Sb_all[:, n, :], Sb, ['Sb'], [('sball', n)])
                pt, pk = nps()
                self.mm(pt[:, 0:128], kbt[:, n, :], vt[:, n, :], True, True, [('kbt', n), 'vt'], [pk])
                self.stt('dve', Sb, Sb, kw[:, h * 4 + 3:h * 4 + 4], pt[:, 0:128], ALU.mult, ALU.add, ['Sb', pk, ('kw', h)], ['Sb'])
            if RS < 6: continue
            for n in range(NCH):
                sl = slice(n * 128, (n + 1) * 128)
                FW = int(_os.environ.get('KDBG_FW', '99'))
                sfb, sfk = Sfb.next()
                self.copy('act', sfb, Sf, ['Sf'], [sfk])
                if FW < 2: continue
                pa, pak = nps()
                self.mm(pa[:, 0:128], kr[:, sl], qr[:, sl], True, True, qrk + krk, [pak])
                sm, smk = smrot.next()
                self.tt('dve', sm, pa[:, 0:128], mask[:, h, :], ALU.mult, [pak, ('mask', h)], [smk])
                if FW < 3: continue
                pr, prk = nps()
                self.mm(pr[:, 0:128], sm, vt[:, n, :], True, False, [smk, 'vt'], [prk])
                self.mm(pr[:, 0:128], qf[:, n, :], sfb, False, False, [('qwfd', n), sfk], [prk])
                self.mm(pr[:, 0:128], qb[:, n, :], Sb_all[:, n, :], False, True, [('qwbd', n), ('sball', n)], [prk])
                if FW < 4: continue
                self.copy('dve', ret_all[:, n, :], pr[:, 0:128], [prk], [('ret', n)])
                self.act(sq_all[:, n, :], ret_all[:, n, :], AF.Square, [('ret', n)], [('sqa', n)])
                if FW < 5: continue
                pd, pdk = nps()
                self.mm(pd[:, 0:128], kft[:, n, :], vt[:, n, :], True, True, [('kft', n), 'vt'], [pdk])
                self.stt('dve', Sf, Sf, kw[:, h * 4 + 2:h * 4 + 3], pd[:, 0:128], ALU.mult, ALU.add, ['Sf', pdk, ('kw', h)], ['Sf'])
            if RS < 7: continue
            if is_ctx:
                self.dma('sp', self.s0[h], Sf, ['Sf'], ['s0'])
                self.dma('sp', self.s0[4 + h], Sb, ['Sb'], ['s0'])
            if RS < 8: continue
            ssk = [('sqa', n) for n in range(NCH)]
            self.P.op('dve', lambda e: e.reduce_sum(ssq, sq_all, axis=AX.X), reads=ssk, writes=['rstd_all'])
            self.rstd(ssq, ssq, 1.0 / 128, ['rstd_all'], 'rstd_all')
            for n in range(NCH):
                rn, rnk = rnrot.next()
                self.act(rn, ret_all[:, n, :], AF.Copy, [('ret', n), 'rstd_all'], [rnk], scale=ssq[:, n:n + 1])
                pt, pk = nps()
                ptb = pt[:].bitcast(BF16)[:, 0:128]
                self.P.op('pe', lambda e, ptb=ptb, rn=rn: e.transpose(ptb, rn, self.ident), reads=[rnk, 'cm'], writes=[pk])
                self.tt('dve', ya[:, n * 128:(n + 1) * 128], ptb, sg[:, n * 128:(n + 1) * 128], ALU.mult, [pk, 'sg'], [('ya', n)])
            self.dma('sp', st['yT'][h * 128:(h + 1) * 128, :], ya, [('ya', n) for n in range(NCH)], ['yT'])
        P.barrier()

    def stage_fourier(self, l, st):
        P = self.P
        T = st['T']; NCH = T // 128
        self.arena_reset()
        fz = self.alloc([128, 4, T], BF16)
        Acs = self.alloc([128, NCH, 4, 256], BF16)
        dfc = self.alloc([128, 256], BF16)
        KT = 256
        trot = self.rot('dt', 2, [128, 2, NCH, KT], BF16)
        self.dma('sp', dfc, self.W['dftc'][:], [], ['dfc'])
        for g in range(4):
            self.dma('sp', fz[:, g, :], st['fT'][g * 128:(g + 1) * 128, :], [], [('fz', g)])
        psi = [0]

        def nps():
            k = psi[0] % 8; psi[0] += 1
            return self.ps[k][:], ('ps', k)
        ei = 0
        for n in range(NCH):
            for g2 in range(2):
                pt, pk = nps()
                for gg in range(2):
                    g = g2 * 2 + gg
                    self.mm(pt[:, gg * 256:(gg + 1) * 256], fz[:, g, n * 128:(n + 1) * 128], dfc, True, True, [('fz', g), 'dfc'], [pk])
                dst = Acs[:, n, g2 * 2:g2 * 2 + 2, :]
                src = pt[:].rearrange("p (a b) -> p a b", a=2)
                self.copy('act' if ei % 2 == 0 else 'dve', dst, src, [pk], [('acs', n)])
                ei += 1
        P.barrier()
        yb = fz
        ack = [('acs', n) for n in range(NCH)]
        for kt in range(T // KT):
            tb_, tk = trot.next()
            for cs in range(2):
                self.dma('sp', tb_[:, cs, :, :], st['dft'][cs][:, kt * KT:(kt + 1) * KT].rearrange("(n p) k -> p n k", p=128), [], [tk])
            for g in range(4):
                pt, pk = nps()
                for n in range(NCH):
                    for cs in range(2):
                        self.mm(pt[:, 0:KT], Acs[:, n, g, cs * 128:(cs + 1) * 128], tb_[:, cs, n, :],
                                n == 0 and cs == 0, n == NCH - 1 and cs == 1, ack + [tk], [pk])
                self.copy('act' if ei % 2 == 0 else 'dve', yb[:, g, kt * KT:(kt + 1) * KT], pt[:, 0:KT], [pk], [('yb', g)])
                ei += 1
        for g in range(4):
            self.dma('sp', st['yT'][512 + g * 128:512 + (g + 1) * 128, :], yb[:, g, :], [('yb', g)], ['yT'])
        P.barrier()

    def stage_sgu(self, l, st):
        P = self.P
        T = st['T']; NCH = T // 128
        self.arena_reset()
        vs = self.alloc([128, NCH, 512], BF16)
        vg = self.alloc([128, NCH, 512], BF16)
        uT = self.alloc([128, 4, T], BF16)
        yc = self.alloc([128, 4, T], BF16)
        wsf = self.alloc([128, 4, 128], F32); wsb = self.alloc([128, 4, 128], BF16)
        bsb = self.alloc([128, 4, 512], F32)
        s1 = self.alloc([128, NCH * 4], F32); s2 = self.alloc([128, NCH * 4], F32); mn = self.alloc([128, NCH * 4], F32)
        PCS = 4
        sqrot = self.rot('sq', 2, [128, PCS * 512], F32)
        tmrot = self.rot('tm', 3, [128, 512], F32)
        self.dma('sp', vs, st['vs_tok'].rearrange("(n p) c -> p n c", p=128), [], ['vs'])
        for g in range(4):
            self.dma('sp', uT[:, g, :], st['uT'][g * 128:(g + 1) * 128, :], [], [('u', g)])
        self.dma('sp', wsf, self.W['sgu_wT'][l], [], ['wsf'])
        self.dma('sp', bsb, self.W['sgu_bb'][l], [], ['bsb'])
        self.copy('dve', wsb, wsf, ['wsf'], ['wsb'])
        for n0 in range(0, NCH, PCS):
            g_ = min(PCS, NCH - n0)
            v3 = vs[:, n0:n0 + g_, :].rearrange("p n (g c) -> p (n g) c", g=4)
            self.P.op('dve', lambda e, v3=v3, n0=n0, g_=g_: e.reduce_sum(s1[:, n0 * 4:(n0 + g_) * 4], v3, axis=AX.X),
                      reads=['vs'], writes=[('s1', n0)])
            sq, sk = sqrot.next()
            sqv = sq[:, 0:g_ * 512]
            self.act(sqv, vs[:, n0:n0 + g_, :].rearrange("p n c -> p (n c)"), AF.Square, ['vs'], [sk])
            self.P.op('dve', lambda e, sqv=sqv, n0=n0, g_=g_: e.reduce_sum(s2[:, n0 * 4:(n0 + g_) * 4],
                                                                           sqv.rearrange("p (a c) -> p a c", c=128), axis=AX.X),
                      reads=[sk], writes=[('s2', n0)])
        s1k = [('s1', n0) for n0 in range(0, NCH, PCS)]; s2k = [('s2', n0) for n0 in range(0, NCH, PCS)]
        self.ts('dve', mn, s1, 1.0 / 128, None, ALU.mult, None, s1k, ['mn'])
        self.tt('dve', s1, mn, mn, ALU.mult, ['mn'], ['msq'])
        self.stt('dve', s2, s2, 1.0 / 128, s1, ALU.mult, ALU.subtract, s2k + ['msq'], ['var'])
        self.rstd(s2, s2, 1.0, ['var'], 'rstdv')
        for n in range(NCH):
            for g in range(4):
                i = n * 4 + g
                self.ts('dve', vg[:, n, g * 128:(g + 1) * 128], vs[:, n, g * 128:(g + 1) * 128],
                        mn[:, i:i + 1], s2[:, i:i + 1], ALU.subtract, ALU.mult, ['vs', 'mn', 'rstdv'], [('vg', n)])
        psi = [0]

        def nps():
            k = psi[0] % 8; psi[0] += 1
            return self.ps[k][:], ('ps', k)
        G = 4
        for g in range(4):
            for n0 in range(0, NCH, G):
                g_ = min(G, NCH - n0)
                pt, pk = nps()
                for n in range(n0, n0 + g_):
                    self.mm(pt[:, (n - n0) * 128:(n - n0 + 1) * 128], vg[:, n, g * 128:(g + 1) * 128], wsb[:, g, :], True, True,
                            [('vg', n), 'wsb'], [pk])
                tm, tk = tmrot.next()
                w_ = g_ * 128
                self.tt('dve', tm[:, 0:w_], pt[:, 0:w_], bsb[:, g, 0:w_], ALU.add, [pk, 'bsb'], [tk])
                self.tt('pool', yc[:, g, n0 * 128:n0 * 128 + w_], tm[:, 0:w_], uT[:, g, n0 * 128:n0 * 128 + w_], ALU.mult,
                        [tk, ('u', g)], [('yc', g)])
            self.dma('sp', st['yT'][1024 + g * 128:1024 + (g + 1) * 128, :], yc[:, g, :], [('yc', g)], ['yT'])
        P.barrier()

    def stage_conv(self, l, st):
        P = self.P
        T = st['T']; RW = st['RW']; NR = T // RW
        self.arena_reset()
        brot = self.rot('cb', 2, [128, T], BF16); crot = self.rot('cc', 2, [128, T], BF16); xrot = self.rot('cx', 2, [128, T], BF16)
        yrot = self.rot('cy', 2, [128, T], F32); orot = self.rot('co', 2, [128, T], F32); drot = self.rot('cd', 2, [128, T], BF16)
        for ct in range(4):
            bt, bk = brot.next(); cg, ck = crot.next(); xv, xk = xrot.next()
            self.dma('sp', bt, st['scT'][ct * 128:(ct + 1) * 128, :], [], [bk])
            self.dma('sp', cg, st['scT'][512 + ct * 128:512 + (ct + 1) * 128, :], [], [ck])
            self.dma('sp', xv, st['scT'][1024 + ct * 128:1024 + (ct + 1) * 128, :], [], [xk])
            y, yk = yrot.next(); o, ok = orot.next(); d, dk = drot.next()
            w0 = self.cw[:, (l * 4 + ct) * 3 + 0:(l * 4 + ct) * 3 + 1]
            w1 = self.cw[:, (l * 4 + ct) * 3 + 1:(l * 4 + ct) * 3 + 2]
            w2 = self.cw[:, (l * 4 + ct) * 3 + 2:(l * 4 + ct) * 3 + 3]
            self.tt('dve', y, cg, xv, ALU.mult, [ck, xk], [yk])
            self.act(o, y, AF.Copy, [yk, 'cw'], [ok], scale=w1)
            y3 = y.rearrange("p (r w) -> p r w", w=RW); o3 = o.rearrange("p (r w) -> p r w", w=RW)
            self.stt('dve', o3[:, :, 1:RW], y3[:, :, 0:RW - 1], w0, o3[:, :, 1:RW], ALU.mult, ALU.add, [yk, ok, 'cw'], [ok])
            self.stt('dve', o3[:, :, 0:RW - 1], y3[:, :, 1:RW], w2, o3[:, :, 0:RW - 1], ALU.mult, ALU.add, [yk, ok, 'cw'], [ok])
            self.tt('dve', d, o, bt, ALU.mult, [ok, bk], [dk])
            self.dma('sp', st['yT'][1536 + ct * 128:1536 + (ct + 1) * 128, :], d, [dk], ['yT'])
        P.barrier()

    def stage_merge(self, l, st):
        P = self.P
        T = st['T']; TW = min(512, T)
        self.arena_reset()
        wbr = self.alloc([128, 16, D], BF16)
        for n in range(4):
            self.dma('pool', wbr[:, n * 4:(n + 1) * 4, :], self.W['w_branch'][l][n * 512:(n + 1) * 512, :].rearrange("(k p) d -> p k d", p=128),
                     [], [('wbr', n)])
        wbk = [('wbr', n) for n in range(4)]
        yrot = self.rot('yt', 2, [128, 16, TW], BF16)
        grot = self.rot('gt', 3, [128, 4, TW], BF16)
        trot = self.rot('t', 12, [128, TW], BF16)
        mrot = self.rot('mg', 2, [128, 16, TW], BF16)
        psi = [0]

        def nps():
            k = psi[0] % 8; psi[0] += 1
            return self.ps[k][:], ('ps', k)
        for t in range(T // TW):
            sl = slice(t * TW, (t + 1) * TW)
            yt, yk = yrot.next()
            self.dma('sp', yt, st['yT'][:, sl].rearrange("(k p) n -> p k n", p=128), ['yT'], [yk])
            mg, mk = mrot.next()
            pend = []

            def flush(keep):
                while len(pend) > keep:
                    tl_, d_ = pend.pop(0)
                    pa_, pak_ = nps()
                    for n_ in range(4):
                        self.mm(pa_[:, 0:TW], self.ident, tl_[n_][0], n_ == 0, n_ == 3, [tl_[n_][1], 'cm'], [pak_])
                    self.act(mg[:, d_, :], pa_[:, 0:TW], AF.Copy, [pak_], [mk])
            for dt_ in range(16):
                gt, gk = grot.next()
                self.dma('sp', gt, st['gateT'][:, sl].rearrange("(n r) t -> r n t", n=4)[dt_ * 128:(dt_ + 1) * 128], [], [gk])
                tl = []
                for n in range(4):
                    pt, pk = nps()
                    for mc in range(4):
                        self.mm(pt[:, 0:TW], wbr[:, n * 4 + mc, dt_ * 128:(dt_ + 1) * 128], yt[:, n * 4 + mc, :],
                                mc == 0, mc == 3, wbk + [yk], [pk])
                    tm, tk = trot.next()
                    self.tt('dve', tm, pt[:, 0:TW], gt[:, n, :], ALU.mult, [pk, gk], [tk])
                    tl.append((tm, tk))
                flush(0)
                pend.append((tl, dt_))
            flush(0)
            self.dma('sp', st['mergedT'][:, sl].rearrange("(k p) n -> p k n", p=128), mg, [mk], ['mergedT'])
        P.barrier()

    def stage_outproj(self, l, st, which):
        P = self.P
        T = st['T']; TW = min(512, T); j = st['j']
        if which == 'mix':
            K = D; wsrc = self.W['w_out'][l]; src = st['mergedT']; G = self.modv[:, 2, :, j]
            res_in = st['res_in']
        else:
            K = DFF; wsrc = self.W['ffn_w_out'][l]; src = st['hT']; G = self.modv[:, 5, :, j]
            res_in = st['res']
        res_out = st['res']
        NKT = K // 128
        self.arena_reset()
        big = (K != D)
        inrot = self.rot('in', 2, [128, NKT, TW], BF16)
        wrot = self.rot('w', 3 if big else 4, [128, NKT, 128], BF16)
        mix = self.alloc([128, 16, TW], F32)
        rrot = self.rot('res', 1 if big else 2, [128, 16, TW], F32)
        sqrot = self.rot('sq', 3, [128, TW], BF16)
        rs = self.alloc([128, TW], F32)
        tmrot = self.rot('tm', 2 if big else 3, [128, TW], F32)
        psi = [0]

        def nps():
            k = 1 + psi[0] % 7; psi[0] += 1
            return self.ps[k][:], ('ps', k)
        nb = T // TW

        def load_in(t):
            sl = slice(t * TW, (t + 1) * TW)
            it, ik = inrot.next()
            self.dma('sp', it, src[:, sl].rearrange("(k p) n -> p k n", p=128), [('src', t)], [ik])
            return it, ik

        def load_res(t):
            sl = slice(t * TW, (t + 1) * TW)
            rt, rk = rrot.next()
            self.dma('sp', rt, res_in[:, sl].rearrange("(k p) n -> p k n", p=128), [('res', t)], [rk])
            return rt, rk
        cur_in = load_in(0); cur_res = load_res(0)
        for t in range(nb):
            sl = slice(t * TW, (t + 1) * TW)
            it, ik = cur_in; rt, rk = cur_res
            ss = self.ps[0][:, 0:TW]
            pend = []

            def flush(keep):
                while len(pend) > keep:
                    sq_, sk_, d_ = pend.pop(0)
                    self.mm(ss, self.ones, sq_, d_ == 0, d_ == 15, [sk_, 'cm'], ['ps0'])
            for dt_ in range(16):
                wt, wk = wrot.next()
                self.dma('pool', wt.rearrange("p k n -> p (k n)"), wsrc[dt_], [], [wk], max_dma_last_dim=8192)
                pt, pk = nps()
                for k in range(NKT):
                    self.mm(pt[:, 0:TW], wt[:, k, :], it[:, k, :], k == 0, k == NKT - 1, [wk, ik], [pk])
                flush(1)
                self.copy('dve', mix[:, dt_, :], pt[:, 0:TW], [pk], [('mix', dt_)])
                sq, sk = sqrot.next()
                self.act(sq, mix[:, dt_, :], AF.Square, [('mix', dt_)], [sk])
                pend.append((sq, sk, dt_))
            flush(0)
            if t + 1 < nb:
                cur_in = load_in(t + 1)
                if not big: cur_res = load_res(t + 1)
            self.rstd(rs, ss, 1.0 / D, ['ps0'], 'rs')
            for dt_ in range(16):
                tm, tk = tmrot.next()
                self.stt('dve', tm, mix[:, dt_, :], G[:, dt_:dt_ + 1], rs, ALU.mult, ALU.mult, [('mix', dt_), 'rs', 'modv'], [tk])
                self.tt('pool' if dt_ % 2 == 0 else 'dve', rt[:, dt_, :], rt[:, dt_, :], tm, ALU.add, [rk, tk], [rk])
            self.dma('sp', res_out[:, sl].rearrange("(k p) n -> p k n", p=128), rt, [rk], [('res', t)])
            if t + 1 < nb and big:
                cur_res = load_res(t + 1)
        P.barrier()


def _bf(a):
    return np.asarray(a, dtype=np.float32).astype(ml_dtypes.bfloat16)


def _rope_tables(T, latent):
    cos = np.ones((128, T), np.float32); sin = np.zeros((128, T), np.float32)
    if latent:
        t = np.arange(T)
        n = 64
        freqs = (np.float32(10000.0) ** (-np.arange(0, n, 2, dtype=np.float32) / np.float32(n))).astype(np.float32)
        for half, pos in ((0, t // 64), (1, t % 64)):
            ang = pos.astype(np.float32)[None, :] * freqs[:, None]
            c = np.cos(ang).astype(np.float32); s = np.sin(ang).astype(np.float32)
            b = half * 64
            cos[b:b + 32] = c; cos[b + 32:b + 64] = c
            sin[b:b + 32] = -s; sin[b + 32:b + 64] = s
    return np.stack([cos, sin]).astype(np.float32)


def _dft_tables(T):
    k = np.arange(T, dtype=np.int64)
    ang = 2.0 * np.pi * ((k[:, None] * k[None, :]) % T).astype(np.float64) / T
    sc = 1.0 / np.sqrt(T * 128.0)
    return np.stack([_bf(np.cos(ang) * sc), _bf(-np.sin(ang) * sc)])


def _consts():
    C = 128
    i = np.arange(C, dtype=np.float32)
    rel = i[None, :] - i[:, None]
    s = np.float32(128 ** -0.5)
    dconst = np.zeros((128, 6 * 128 + 4), np.float32)
    dconst[:, 0:128] = np.maximum(rel, 0)
    dconst[:, 128:256] = (rel >= 0).astype(np.float32) * s
    dconst[:, 256:384] = np.maximum(-rel, 0)
    dconst[:, 384:512] = (rel <= 0).astype(np.float32) * s
    dconst[:, 512:640] = (i + 1)[None, :]
    dconst[:, 640:768] = (C - i)[None, :]
    dconst[:, 768] = C - 1 - i
    dconst[:, 769] = i
    dconst[:, 770] = C
    ident = np.eye(128, dtype=np.float32)
    perm = np.zeros((128, 128), np.float32)
    for d in range(128):
        partner = d + 32 if (d % 64) < 32 else d - 32
        perm[partner, d] = 1.0
    cmat = _bf(np.concatenate([ident, perm, np.ones((128, 128), np.float32)], axis=1))
    c = np.arange(128, dtype=np.int64)
    ang = 2.0 * np.pi * ((c[:, None] * c[None, :]) % 128).astype(np.float64) / 128
    dftc = _bf(np.concatenate([np.cos(ang), np.sin(ang)], axis=1))
    return dconst, cmat, dftc


def prepare_inputs(x, c, ctx, c_ctx, ada_w, ada_b, norm_g, w_in, ret_log_decay, conv_w, sgu_w, sgu_b,
                   w_branch, w_out, ffn_w_in, ffn_w_out, cores=range(NCORES), depth=DEPTH):
    f = lambda a: np.ascontiguousarray(np.asarray(a, dtype=np.float32))
    x = np.asarray(x); ctx = np.asarray(ctx); c = np.asarray(c); c_ctx = np.asarray(c_ctx)
    dconst, cmat, dftc = _consts()
    shared = {
        "ada_w": f(np.asarray(ada_w)[:depth]),
        "ada_b": f(np.asarray(ada_b).reshape(DEPTH, 96, 128).transpose(0, 2, 1)[:depth]),
        "norm_g": f(np.asarray(norm_g).reshape(DEPTH, 4, 16, 128).transpose(3, 0, 1, 2).reshape(128, -1)),
        "w_in": f(np.asarray(w_in)[:depth]),
        "lgb": f(np.broadcast_to(np.asarray(ret_log_decay).reshape(1, DEPTH * 8), (128, DEPTH * 8))),
        "conv_w": f(np.asarray(conv_w).reshape(DEPTH, 3, 4, 128).transpose(3, 0, 2, 1).reshape(128, -1)),
        "sgu_wT": f(np.asarray(sgu_w).transpose(0, 3, 1, 2)[:depth]),
        "sgu_bb": f(np.broadcast_to(np.asarray(sgu_b)[:, None, :, None, :], (DEPTH, 128, 4, 4, 128)).reshape(DEPTH, 128, 4, 512)[:depth]),
        "w_branch": f(np.asarray(w_branch).reshape(DEPTH, 4 * MIXW, D)[:depth]),
        "w_out": f(np.asarray(w_out)[:depth].reshape(depth, 16, 128, 16, 128).transpose(0, 3, 2, 1, 4).reshape(depth, 16, 128, D)),
        "ffn_w_in": f(np.asarray(ffn_w_in)[:depth]),
        "ffn_w_out": f(np.asarray(ffn_w_out)[:depth].reshape(depth, 44, 128, 16, 128).transpose(0, 3, 2, 1, 4).reshape(depth, 16, 128, DFF)),
        "dconst": dconst, "cmat": cmat, "dftc": dftc,
        "rope_l": _rope_tables(SEQ, True), "rope_c": _rope_tables(CTX, False),
        "dft_l": _dft_tables(SEQ), "dft_cx": _dft_tables(CTX),
    }
    in_maps = []
    for b in cores:
        m = dict(shared)
        m["xT"] = f(x[b].T)
        m["ctxT"] = f(ctx[b].T)
        cc2 = np.stack([c[b], c_ctx], axis=1).astype(np.float32)
        m["cc"] = f(cc2.reshape(16, 128, 2).transpose(1, 0, 2))
        in_maps.append(m)
    return in_maps


_NC_CACHE = {}


def kernel(**inputs):
    if 'nc' not in _NC_CACHE:
        _NC_CACHE['nc'] = Builder().build()
    nc = _NC_CACHE['nc']
    in_maps = prepare_inputs(**inputs)
    res = run_bass_kernel_spmd(nc, in_maps, core_ids=list(range(NCORES)))
    out = np.stack([np.asarray(r["outT"]).T for r in res.results], axis=0)
    return np.ascontiguousarray(out.astype(np.float32))
```

```python
import numpy as np
import ml_dtypes
import concourse.bass as bass
import concourse.mybir as mybir
from concourse.bass_utils import run_bass_kernel_spmd

F32 = mybir.dt.float32
BF16 = mybir.dt.bfloat16
U8 = mybir.dt.uint8
AF = mybir.ActivationFunctionType
ALU = mybir.AluOpType
AX = mybir.AxisListType

D = 2048
SEQ = 4096
CTX = 256
DEPTH = 4
MIXW = 512
DFF = 5632
INW = 13312
EPS = 1e-6
NCORES = 8

ENGS = ['pe', 'act', 'dve', 'pool', 'sp']
EIDX = {e: i for i, e in enumerate(ENGS)}
NDMA_SEM = 10
DMA_ENGS = ['sp', 'pool', 'act']
NK = len(ENGS) + len(DMA_ENGS) * NDMA_SEM


class Op:
    __slots__ = ('eng', 'fn', 'reads', 'writes', 'dma', 'deps', 'sig', 'sem', 'val', 'idx', 'clock', 'kidx')

    def __init__(s, eng, fn, reads, writes, dma):
        s.eng = eng; s.fn = fn; s.reads = reads; s.writes = writes; s.dma = dma
        s.deps = []; s.sig = False; s.sem = None; s.val = 0; s.clock = None; s.kidx = EIDX[eng]


class Prog:
    def __init__(self, nc):
        self.nc = nc
        self.ops = []
        self.lastw = {}
        self.readers = {}
        self.last_on = {e: None for e in ENGS}
        self.bar_deps = []
        self.bar_pending = set()
        self.dma_cnt = {e: 0 for e in DMA_ENGS}
        self.dma_last = {}

    def op(self, eng, fn, reads=(), writes=(), dma=False):
        pr_ = [r for r in reads if r == 'ps0' or (isinstance(r, tuple) and r[0] == 'ps')]
        if pr_:
            writes = tuple(writes) + tuple(r for r in pr_ if r not in writes)
        o = Op(eng, fn, tuple(reads), tuple(writes), dma)
        o.idx = len(self.ops)
        deps = set()
        if eng in self.bar_pending:
            deps.update(self.bar_deps); self.bar_pending.discard(eng)
        lastw = self.lastw
        for r in o.reads:
            w = lastw.get(r)
            if w is not None: deps.add(w)
        for r in o.writes:
            w = lastw.get(r)
            if w is not None: deps.add(w)
            rs = self.readers.get(r)
            if rs: deps.update(rs.values())
        if dma:
            k = self.dma_cnt[eng]; self.dma_cnt[eng] += 1
            slot = k % NDMA_SEM
            o.kidx = len(ENGS) + DMA_ENGS.index(eng) * NDMA_SEM + slot
            o.val = 16 * (k // NDMA_SEM + 1)
            prev = self.dma_last.get(o.kidx)
            if prev is not None: deps.add(prev)
            self.dma_last[o.kidx] = o.idx
            o.sig = True
        ops = self.ops
        for d in deps:
            p = ops[d]
            if p.eng == eng and not p.dma and not dma:
                if eng == 'pe': continue
            o.deps.append(d)
        rk = ('d', o.idx) if dma else eng
        for r in o.reads:
            self.readers.setdefault(r, {})[rk] = o.idx
        for r in o.writes:
            lastw[r] = o.idx
            self.readers[r] = {}
        ops.append(o)
        self.last_on[eng] = o.idx
        return o

    def barrier(self):
        deps = set()
        for e in ENGS:
            if self.last_on[e] is not None: deps.add(self.last_on[e])
        deps.update(self.dma_last.values())
        self.bar_deps = sorted(deps)
        self.bar_pending = set(ENGS)
        self.lastw = {}; self.readers = {}

    def finish(self):
        self.barrier()
        self.op('sp', lambda e: e.nop())

    def emit(self, new_sem):
        ops = self.ops
        for o in ops:
            for d in o.deps: ops[d].sig = True
        sems = [new_sem(f"k{i}") for i in range(NK)]
        cnt = [0] * len(ENGS)
        for o in ops:
            if not o.dma:
                if o.sig: cnt[o.kidx] += 1
                o.val = cnt[o.kidx]
            o.sem = sems[o.kidx]
        self.sig_counts = cnt
        know = {e: [0] * NK for e in ENGS}
        streams = {e: [] for e in ENGS}
        for o in ops:
            kc = know[o.eng]
            waits = {}
            for d in o.deps:
                p = ops[d]
                if kc[p.kidx] >= p.val: continue
                if waits.get(p.kidx, 0) < p.val: waits[p.kidx] = p.val
            for d in o.deps:
                p = ops[d]
                if kc[p.kidx] >= p.val: continue
                pc = p.clock
                if pc is not None:
                    for i in range(NK):
                        if pc[i] > kc[i]: kc[i] = pc[i]
                if kc[p.kidx] < p.val: kc[p.kidx] = p.val
            if o.sig:
                c2 = list(kc)
                if c2[o.kidx] < o.val: c2[o.kidx] = o.val
                o.clock = c2
            streams[o.eng].append((o, [(sems[k], v) for k, v in waits.items()]))
        return streams


def run_streams(nc, streams):
    engmap = {'pe': 'tensor', 'act': 'scalar', 'dve': 'vector', 'pool': 'gpsimd', 'sp': 'sync'}
    with nc.Block() as block:
        for e in ENGS:
            lst = streams[e]

            def body(eng, lst=lst):
                for o, waits in lst:
                    for s, v in waits:
                        eng.wait_ge(s, v)
                    ins = o.fn(eng)
                    if o.sig:
                        ins.then_inc(o.sem, 16 if o.dma else 1)
            getattr(block, engmap[e])(body)


ARENA_BYTES = 196 * 1024


class Rot:
    def __init__(self, name, aps):
        self.name = name; self.aps = aps; self.i = 0

    def next(self):
        k = self.i % len(self.aps); self.i += 1
        return self.aps[k], (self.name, k)


class Builder:
    def __init__(self, depth=DEPTH, taps=(), stop_after=None):
        self.depth = depth
        self.taps = set(taps)
        self.stop_after = stop_after
        nc = self.nc = bass.Bass("TRN2", target_bir_lowering=False)
        self.P = Prog(nc)
        self.din = {}
        self._uid = 0

    def inp(self, name, shape, dt):
        t = self.nc.dram_tensor(name, list(shape), dt, kind="ExternalInput").ap()
        self.din[name] = t
        return t

    def scratch(self, name, shape, dt):
        kind = "ExternalOutput" if name in self.taps else "Internal"
        return self.nc.dram_tensor(name, list(shape), dt, kind=kind).ap()

    def arena_reset(self):
        self.aoff = 0

    def alloc(self, shape, dt):
        esz = 4 if dt == F32 else 2
        n = int(np.prod(shape[1:])) * esz
        n_al = (n + 63) // 64 * 64
        assert self.aoff + n_al <= ARENA_BYTES, (self.aoff, n_al)
        ap = self.arena[:, self.aoff:self.aoff + n].bitcast(dt)
        self.aoff += n_al
        if len(shape) == 3:
            ap = ap.rearrange("p (a b) -> p a b", a=shape[1])
        elif len(shape) == 4:
            ap = ap.rearrange("p (a b c) -> p a b c", a=shape[1], b=shape[2])
        return ap

    def rot(self, name, n, shape, dt):
        return Rot(name, [self.alloc(shape, dt) for _ in range(n)])

    def uid(self, s):
        self._uid += 1
        return f"{s}{self._uid}"

    def dma(self, q, out, in_, reads, writes, **kw):
        self.P.op(q, lambda e: e.dma_start(out=out, in_=in_, **kw), reads=reads, writes=writes, dma=True)

    def mm(self, out, lhsT, rhs, start, stop, reads, writes):
        self.P.op('pe', lambda e: e.matmul(out, lhsT, rhs, start=start, stop=stop), reads=reads, writes=writes)

    def act(self, out, in_, func, reads, writes, scale=1.0, bias=None, accum_out=None):
        kw = {}
        if bias is not None: kw['bias'] = bias
        if accum_out is not None: kw['accum_out'] = accum_out
        self.P.op('act', lambda e: e.activation(out, in_, func, scale=scale, **kw), reads=reads, writes=writes)

    def tt(self, eng, out, in0, in1, op, reads, writes):
        self.P.op(eng, lambda e: e.tensor_tensor(out, in0, in1, op), reads=reads, writes=writes)

    def ts(self, eng, out, in0, s1, s2, op0, op1, reads, writes):
        if op1 is None:
            self.P.op(eng, lambda e: e.tensor_scalar(out, in0, s1, None, op0), reads=reads, writes=writes)
        else:
            self.P.op(eng, lambda e: e.tensor_scalar(out, in0, s1, s2, op0, op1), reads=reads, writes=writes)

    def stt(self, eng, out, in0, scalar, in1, op0, op1, reads, writes):
        self.P.op(eng, lambda e: e.scalar_tensor_tensor(out, in0, scalar, in1, op0, op1), reads=reads, writes=writes)

    def copy(self, eng, out, in_, reads, writes):
        if eng == 'act':
            self.act(out, in_, AF.Copy, reads, writes)
        else:
            self.P.op(eng, lambda e: e.tensor_copy(out, in_), reads=reads, writes=writes)

    def cast_wo(self, l, dt_, src=None):
        src = self.W['ffn_w_out'] if src is None else src
        self.dma('pool', self.wo_bf[l, dt_], src[l, dt_], [], [self.uid('wocast')], max_dma_last_dim=8192)

    def rstd(self, out, ss, inv_n, reads, key):
        self.ts('dve', out, ss, inv_n, EPS, ALU.mult, ALU.add, reads, [key])
        self.act(out, out, AF.Sqrt, [key], [key])
        self.P.op('dve', lambda e: e.reciprocal(out, out), reads=[key], writes=[key])

    def build(self):
        nc = self.nc
        depth = self.depth
        xT = self.inp("xT", [D, SEQ], F32)
        ctxT = self.inp("ctxT", [D, CTX], F32)
        cc = self.inp("cc", [128, 16, 2], F32)
        DEPTH = self.depth
        ada_w = self.inp("ada_w", [DEPTH, D, 6 * D], F32)
        ada_b = self.inp("ada_b", [DEPTH, 128, 96], F32)
        norm_g = self.inp("norm_g", [128, 4 * 4 * 16], F32)
        w_in = self.inp("w_in", [DEPTH, D, INW], F32)
        lgb = self.inp("lgb", [128, 4 * 8], F32)
        conv_w = self.inp("conv_w", [128, 4 * 4 * 3], F32)
        sgu_wT = self.inp("sgu_wT", [DEPTH, 128, 4, 128], F32)
        sgu_bb = self.inp("sgu_bb", [DEPTH, 128, 4, 512], F32)
        w_branch = self.inp("w_branch", [DEPTH, 4 * MIXW, D], F32)
        w_out = self.inp("w_out", [DEPTH, 16, 128, D], F32)
        ffn_w_in = self.inp("ffn_w_in", [DEPTH, D, 2 * DFF], F32)
        ffn_w_out = self.inp("ffn_w_out", [DEPTH, 16, 128, DFF], F32)
        dconst = self.inp("dconst", [128, 6 * 128 + 4], F32)
        cmat = self.inp("cmat", [128, 3 * 128], BF16)
        dftc = self.inp("dftc", [128, 256], BF16)
        rope_l = self.inp("rope_l", [2, 128, SEQ], F32)
        rope_c = self.inp("rope_c", [2, 128, CTX], F32)
        dft_l = self.inp("dft_l", [2, SEQ, SEQ], BF16)
        dft_cx = self.inp("dft_cx", [2, CTX, CTX], BF16)
        outT = nc.dram_tensor("outT", [D, SEQ], F32, kind="ExternalOutput").ap()

        def mkstream(nm, T):
            s = dict(name=nm, T=T)
            s['qkT'] = self.scratch(f"{nm}_qkT", [1024, T], BF16)
            s['v_tok'] = self.scratch(f"{nm}_vtok", [T, 512], BF16)
            s['sgT'] = self.scratch(f"{nm}_sgT", [512, T], BF16)
            s['fT'] = self.scratch(f"{nm}_fT", [512, T], BF16)
            s['uT'] = self.scratch(f"{nm}_uT", [512, T], BF16)
            s['vs_tok'] = self.scratch(f"{nm}_vstok", [T, 512], BF16)
            s['scT'] = self.scratch(f"{nm}_scT", [1536, T], BF16)
            s['gateT'] = self.scratch(f"{nm}_gateT", [8192, T], BF16)
            s['yT'] = self.scratch(f"{nm}_yT", [2048, T], BF16)
            s['mergedT'] = self.scratch(f"{nm}_mergedT", [2048, T], BF16)
            s['hT'] = self.scratch(f"{nm}_hT", [DFF, T], BF16)
            return s
        lat = mkstream("lat", SEQ)
        lat.update(j=0, RW=64, rope=rope_l, dft=dft_l, TB=2048, res_in=xT, res=outT)
        ctx = mkstream("ctx", CTX)
        ctxres = self.scratch("ctx_res", [D, CTX], F32)
        ctx.update(j=1, RW=CTX, rope=rope_c, dft=dft_cx, TB=CTX, res_in=ctxT, res=ctxres)
        self.s0 = self.scratch("s0", [8, 128, 128], F32)
        self.wo_bf = self.scratch("wo_bf", [DEPTH, 16, 128, DFF], BF16)

        self.modv = nc.alloc_sbuf_tensor("modv", [128, 6, 16, 2], F32)
        self.ng = nc.alloc_sbuf_tensor("ng", [128, 4 * 4 * 16], F32)
        self.lg = nc.alloc_sbuf_tensor("lg", [128, 4 * 8], F32)
        self.cw = nc.alloc_sbuf_tensor("cw", [128, 4 * 4 * 3], F32)
        self.dc = nc.alloc_sbuf_tensor("dc", [128, 6 * 128 + 4], F32)
        self.cm = nc.alloc_sbuf_tensor("cm", [128, 3 * 128], BF16)
        self.scc = nc.alloc_sbuf_tensor("scc", [128, 16, 2], BF16)
        self.ccf = nc.alloc_sbuf_tensor("ccf", [128, 16, 2], F32)
        self.arena = nc.alloc_sbuf_tensor("arena", [128, ARENA_BYTES], U8)
        self.ps = [nc.alloc_psum_tensor(f"ps{i}", [128, 512], F32) for i in range(8)]
        self.ident = self.cm[:, 0:128]
        self.perm = self.cm[:, 128:256]
        self.ones = self.cm[:, 256:384]

        P = self.P
        for (dst, src, nm) in ((self.ng, norm_g, 'ng'), (self.lg, lgb, 'lg'), (self.cw, conv_w, 'cw'),
                               (self.dc, dconst, 'dc'), (self.cm, cmat, 'cm')):
            self.dma('sp', dst[:], src[:], [], [nm])
        self.dma('sp', self.ccf[:], cc[:], [], ['ccf'])
        self.act(self.scc[:], self.ccf[:], AF.Silu, ['ccf'], ['scc'])
        for dt_ in range(16):
            self.cast_wo(0, dt_, ffn_w_out)
        P.barrier()

        W = dict(ada_w=ada_w, ada_b=ada_b, w_in=w_in, sgu_wT=sgu_wT, sgu_bb=sgu_bb, w_branch=w_branch,
                 w_out=w_out, ffn_w_in=ffn_w_in, ffn_w_out=ffn_w_out, dftc=dftc)
        self.W = W
        done = False
        for l in range(depth):
            last = (l == 3)
            self.stage_adaln(l)
            for st in (ctx, lat):
                def stop(nm):
                    return self.stop_after == st['name'] + '_' + nm
                if self.stop_after == 'adaln': done = True; break
                self.stage_gemm_in(l, st, which='inproj')
                if stop('inproj'): done = True; break
                self.stage_retention(l, st)
                if stop('ret'): done = True; break
                if st is ctx and last:
                    continue
                self.stage_fourier(l, st)
                if stop('fourier'): done = True; break
                self.stage_sgu(l, st)
                if stop('sgu'): done = True; break
                self.stage_conv(l, st)
                if stop('conv'): done = True; break
                self.stage_merge(l, st)
                if stop('merge'): done = True; break
                self.stage_outproj(l, st, which='mix')
                if stop('sub1'): done = True; break
                self.stage_gemm_in(l, st, which='ffn1')
                if stop('ffn1'): done = True; break
                self.stage_outproj(l, st, which='ffn2')
                if stop('ffn2'): done = True; break
                st['res_in'] = st['res']
            if done: break
        P.finish()
        streams = P.emit(lambda name: nc.alloc_semaphore(name))
        run_streams(nc, streams)
        return nc

    def stage_adaln(self, l):
        P = self.P
        self.arena_reset()
        wrot = self.rot('aw', 3, [128, 16, 512], BF16)
        ab = self.alloc([128, 96], F32)
        mod = self.alloc([128, 96, 2], F32)
        self.dma('sp', ab, self.W['ada_b'][l], [], ['ab'])
        pst = self.ps[0][:]
        psv = pst[:, 0:192].rearrange("p (a b) -> p a b", b=2)
        for c in range(24):
            wt, wk = wrot.next()
            self.dma('pool', wt, self.W['ada_w'][l][:, c * 512:(c + 1) * 512].rearrange("(k p) n -> p k n", p=128), [], [wk])
            for m in range(4):
                ct = c * 4 + m
                for k in range(16):
                    self.mm(pst[:, ct * 2:ct * 2 + 2], wt[:, k, m * 128:(m + 1) * 128], self.scc[:, k, :],
                            k == 0, k == 15, [wk, 'scc'], ['ps0'])
        for j in range(2):
            self.tt('dve', mod[:, :, j], psv[:, :, j], ab, ALU.add, ['ps0', 'ab'], ['mod'])
        ngl = self.ng[:, l * 64:(l + 1) * 64].rearrange("p (i k) -> p i k", i=4)
        mv = self.modv
        for j in range(2):
            self.stt('dve', mv[:, 0, :, j], mod[:, 16:32, j], 1.0, ngl[:, 0, :], ALU.add, ALU.mult, ['mod', 'ng'], ['modv'])
            self.copy('dve', mv[:, 1, :, j], mod[:, 0:16, j], ['mod'], ['modv'])
            self.tt('dve', mv[:, 2, :, j], mod[:, 32:48, j], ngl[:, 1, :], ALU.mult, ['mod', 'ng'], ['modv'])
            self.stt('dve', mv[:, 3, :, j], mod[:, 64:80, j], 1.0, ngl[:, 2, :], ALU.add, ALU.mult, ['mod', 'ng'], ['modv'])
            self.copy('dve', mv[:, 4, :, j], mod[:, 48:64, j], ['mod'], ['modv'])
            self.tt('dve', mv[:, 5, :, j], mod[:, 80:96, j], ngl[:, 3, :], ALU.mult, ['mod', 'ng'], ['modv'])
        if 'tap_modv' in self.taps:
            tap = self.scratch('tap_modv', [128, 192], F32)
            self.dma('sp', tap, mv[:].rearrange("p a b c -> p (a b c)"), ['modv'], ['tapm'])
        P.barrier()

    def stage_gemm_in(self, l, st, which):
        P = self.P
        T = st['T']; TB = st['TB']; j = st['j']
        TT = 256
        mi = 0 if which == 'inproj' else 3
        A = self.modv[:, mi, :, j]; S = self.modv[:, mi + 1, :, j]
        src = st['res_in'] if which == 'inproj' else st['res']
        for tb in range(T // TB):
            self.arena_reset()
            hx = self.alloc([128, 16, TB], BF16)
            xrot = self.rot('xin', 2, [128, 16, TT], F32)
            sqrot = self.rot('sq', 4, [128, TT], BF16)
            rsrot = self.rot('rs', 2, [128, TT], F32)
            tmrot = self.rot('tm', 3, [128, 512], F32)
            nw = 3 if which == 'inproj' else 4
            wrot = self.rot('w', nw, [128, 16, 512], BF16)
            orot = self.rot('ob', 3, [128, TB], BF16)
            otrot = self.rot('obt', 3, [128, 512], BF16)
            psi = [0]

            def nps():
                k = 1 + psi[0] % 7; psi[0] += 1
                return self.ps[k][:], ('ps', k)
            for t in range(TB // TT):
                t0 = tb * TB + t * TT
                xt, xk = xrot.next()
                self.dma('sp', xt, src[:, t0:t0 + TT].rearrange("(k p) n -> p k n", p=128), ['res'], [xk])
                ss = self.ps[0][:, 0:TT]
                for k in range(16):
                    sq, sk = sqrot.next()
                    self.act(sq, xt[:, k, :], AF.Square, [xk], [sk])
                    self.mm(ss, self.ones, sq, k == 0, k == 15, [sk], ['ps0'])
                rs, rk = rsrot.next()
                self.rstd(rs, ss, 1.0 / D, ['ps0'], rk)
                for k in range(16):
                    tm, tk = tmrot.next()
                    tmv = tm[:, 0:TT]
                    self.stt('dve', tmv, xt[:, k, :], A[:, k:k + 1], rs, ALU.mult, ALU.mult, [xk, rk, 'modv'], [tk])
                    self.act(hx[:, k, t * TT:(t + 1) * TT], tmv, AF.Identity, [tk, 'modv'], [('hx', t)], bias=S[:, k:k + 1])
            hxr = [('hx', t) for t in range(TB // TT)]
            import os as _os
            if _os.environ.get('KDBG_SKIP_GEMM'):
                P.barrier(); continue
            if which == 'inproj':
                wsrc = self.W['w_in'][l]
                plan = []
                plan.append((0, 'B', AF.Copy, st['qkT'], 0))
                plan.append((1, 'B', AF.Copy, st['qkT'], 512))
                plan.append((2, 'A', AF.Copy, st['v_tok'], 0))
                plan.append((4, 'B', AF.Copy, st['fT'], 0))
                for c in (7, 8, 9):
                    plan.append((c, 'B', AF.Copy, st['scT'], (c - 7) * 512))
                plan.append((3, 'B', AF.Silu, st['sgT'], 0))
                plan.append((5, 'B', AF.Gelu_apprx_tanh, st['uT'], 0))
                plan.append((6, 'A', AF.Gelu_apprx_tanh, st['vs_tok'], 0))
                for c in range(10, 26):
                    plan.append((c, 'B', AF.Sigmoid, st['gateT'], (c - 10) * 512))
                ei = 0
                if _os.environ.get('KDBG_COPYONLY'):
                    plan = [(c, ori, AF.Copy, dst, off) for (c, ori, func, dst, off) in plan]
                if _os.environ.get('KDBG_PLAN'):
                    plan = [plan[int(i_)] for i_ in _os.environ['KDBG_PLAN'].split(',')]
                if _os.environ.get('KDBG_NOA'):
                    plan = [p_ for p_ in plan if p_[1] == 'B']
                if _os.environ.get('KDBG_ONLYA'):
                    plan = [p_ for p_ in plan if p_[1] == 'A']
                for (c, ori, func, dst, off) in plan:
                    wt, wk = wrot.next()
                    self.dma('pool', wt, wsrc[:, c * 512:(c + 1) * 512].rearrange("(k p) n -> p k n", p=128), [], [wk])
                    if ori == 'B':
                        for m in range(4):
                            ob, ok = orot.next()
                            for t in range(TB // 512 if TB >= 512 else 1):
                                tw = min(512, TB)
                                pt, pk = nps()
                                for k in range(16):
                                    self.mm(pt[:, 0:tw], wt[:, k, m * 128:(m + 1) * 128], hx[:, k, t * tw:(t + 1) * tw],
                                            k == 0, k == 15, [wk] + hxr, [pk])
                                if func == AF.Copy and ei % 2 == 1:
                                    self.copy('dve', ob[:, t * tw:(t + 1) * tw], pt[:, 0:tw], [pk], [ok])
                                else:
                                    self.act(ob[:, t * tw:(t + 1) * tw], pt[:, 0:tw], func, [pk], [ok])
                                ei += 1
                            r0 = off + m * 128
                            self.dma('sp', dst[r0:r0 + 128, tb * TB:(tb + 1) * TB], ob, [ok], [self.uid('st')])
                    else:
                        for tt_ in range(TB // 128):
                            pt, pk = nps()
                            for k in range(16):
                                self.mm(pt, hx[:, k, tt_ * 128:(tt_ + 1) * 128], wt[:, k, :], k == 0, k == 15, [wk] + hxr, [pk])
                            ob, ok = otrot.next()
                            if func == AF.Copy and ei % 2 == 1:
                                self.copy('dve', ob, pt, [pk], [ok])
                            else:
                                self.act(ob, pt, func, [pk], [ok])
                            ei += 1
                            r0 = tb * TB + tt_ * 128
                            self.dma('sp', dst[r0:r0 + 128, :], ob, [ok], [self.uid('st')])
            else:
                wsrc = self.W['ffn_w_in'][l]
                for c in range(DFF // 512):
                    if st['name'] == 'lat' and l + 1 < self.depth:
                        for dt_ in range(16):
                            if dt_ * (T // TB) * (DFF // 512) // 16 == tb * (DFF // 512) + c:
                                self.cast_wo(l + 1, dt_)
                    wa, wak = wrot.next()
                    self.dma('pool', wa, wsrc[:, c * 512:(c + 1) * 512].rearrange("(k p) n -> p k n", p=128), [], [wak])
                    wb, wbk = wrot.next()
                    self.dma('pool', wb, wsrc[:, DFF + c * 512:DFF + (c + 1) * 512].rearrange("(k p) n -> p k n", p=128), [], [wbk])
                    for m in range(4):
                        ob, ok = orot.next()
                        for t in range(TB // 512 if TB >= 512 else 1):
                            tw = min(512, TB)
                            pa, pak = nps()
                            for k in range(16):
                                self.mm(pa[:, 0:tw], wa[:, k, m * 128:(m + 1) * 128], hx[:, k, t * tw:(t + 1) * tw],
                                        k == 0, k == 15, [wak] + hxr, [pak])
                            pb, pbk = nps()
                            for k in range(16):
                                self.mm(pb[:, 0:tw], wb[:, k, m * 128:(m + 1) * 128], hx[:, k, t * tw:(t + 1) * tw],
                                        k == 0, k == 15, [wbk] + hxr, [pbk])
                            tm, tk = tmrot.next()
                            self.act(tm[:, 0:tw], pa[:, 0:tw], AF.Silu, [pak], [tk])
                            self.tt('dve', ob[:, t * tw:(t + 1) * tw], tm[:, 0:tw], pb[:, 0:tw], ALU.mult, [tk, pbk], [ok])
                        r0 = c * 512 + m * 128
                        self.dma('sp', st['hT'][r0:r0 + 128, tb * TB:(tb + 1) * TB], ob, [ok], [self.uid('st')])
            P.barrier()

    def stage_retention(self, l, st):
        P = self.P
        T = st['T']; NCH = T // 128
        is_ctx = st['name'] == 'ctx'
        self.arena_reset()
        dc = self.dc
        RELP = dc[:, 0:128]; MPs = dc[:, 128:256]; RELN = dc[:, 256:384]; MNs = dc[:, 384:512]
        IR1 = dc[:, 512:640]; CI = dc[:, 640:768]
        CJ1 = dc[:, 768:769]; JJ = dc[:, 769:770]; CC = dc[:, 770:771]
        SC = float(128 ** -0.5)
        cosT = self.alloc([128, T], F32); sinT = self.alloc([128, T], F32)
        self.dma('sp', cosT, st['rope'][0], [], ['cos'])
        self.dma('sp', sinT, st['rope'][1], [], ['sin'])
        mask = self.alloc([128, 4, 128], F32); mtmp = self.alloc([128, 128], F32)
        QWf = self.alloc([128, 4, 128], F32); QWb = self.alloc([128, 4, 128], F32)
        kw = self.alloc([128, 16], F32)
        for h in range(4):
            lgf = self.lg[:, l * 8 + h:l * 8 + h + 1]; lgbk = self.lg[:, l * 8 + 4 + h:l * 8 + 4 + h + 1]
            mk = ('mask', h)
            self.act(mask[:, h, :], RELP, AF.Exp, ['dc', 'lg'], [mk], scale=lgf)
            self.tt('dve', mask[:, h, :], mask[:, h, :], MPs, ALU.mult, [mk, 'dc'], [mk])
            self.act(mtmp, RELN, AF.Exp, ['dc', 'lg'], ['mtmp'], scale=lgbk)
            self.tt('dve', mtmp, mtmp, MNs, ALU.mult, ['mtmp', 'dc'], ['mtmp'])
            self.tt('dve', mask[:, h, :], mask[:, h, :], mtmp, ALU.add, [mk, 'mtmp'], [mk])
            self.act(QWf[:, h, :], IR1, AF.Exp, ['dc', 'lg'], [('qwf', h)], scale=lgf)
            self.ts('dve', QWf[:, h, :], QWf[:, h, :], SC, None, ALU.mult, None, [('qwf', h)], [('qwf', h)])
            self.act(QWb[:, h, :], CI, AF.Exp, ['dc', 'lg'], [('qwb', h)], scale=lgbk)
            self.ts('dve', QWb[:, h, :], QWb[:, h, :], SC, None, ALU.mult, None, [('qwb', h)], [('qwb', h)])
            self.act(kw[:, h * 4 + 0:h * 4 + 1], CJ1, AF.Exp, ['dc', 'lg'], [('kw', h)], scale=lgf)
            self.act(kw[:, h * 4 + 1:h * 4 + 2], JJ, AF.Exp, ['dc', 'lg'], [('kw', h)], scale=lgbk)
            self.act(kw[:, h * 4 + 2:h * 4 + 3], CC, AF.Exp, ['dc', 'lg'], [('kw', h)], scale=lgf)
            self.act(kw[:, h * 4 + 3:h * 4 + 4], CC, AF.Exp, ['dc', 'lg'], [('kw', h)], scale=lgbk)
        qraw = self.alloc([128, T], BF16); kraw = self.alloc([128, T], BF16)
        qr = self.alloc([128, T], BF16); kr = self.alloc([128, T], BF16)
        qf = self.alloc([128, NCH, 128], BF16); qb = self.alloc([128, NCH, 128], BF16)
        kft = self.alloc([128, NCH, 128], BF16); kbt = self.alloc([128, NCH, 128], BF16)
        vt = self.alloc([128, NCH, 128], BF16)
        sg = self.alloc([128, T], BF16)
        ya = self.alloc([128, T], BF16)
        Sb_all = self.alloc([128, NCH, 128], BF16)
        Sf = self.alloc([128, 128], F32); Sb = self.alloc([128, 128], F32)
        Sfb = self.rot('sfb', 2, [128, 128], BF16)
        ret_all = self.alloc([128, NCH, 128], F32)
        ssq = self.alloc([128, NCH], F32)
        sq_all = self.alloc([128, NCH, 128], F32)
        smrot = self.rot('sm', 3, [128, 128], BF16)
        rnrot = self.rot('rn', 3, [128, 128], BF16)
        t1rot = self.rot('t1', 2, [128, 512], F32)
        t2rot = self.rot('t2', 2, [128, 512], F32)
        psi = [0]

        def nps():
            k = psi[0] % 8; psi[0] += 1
            return self.ps[k][:], ('ps', k)
        import os as _os
        RS = int(_os.environ.get('KDBG_RET', '99'))
        for h in range(4):
            if RS < 2: break
            hk = lambda s: (s, h)
            self.dma('sp', qraw, st['qkT'][h * 128:(h + 1) * 128, :], [], ['qraw'])
            self.dma('sp', kraw, st['qkT'][512 + h * 128:512 + (h + 1) * 128, :], [], ['kraw'])
            self.dma('sp', vt, st['v_tok'][:, h * 128:(h + 1) * 128].rearrange("(n p) e -> p n e", p=128), [], ['vt'])
            self.dma('sp', sg, st['sgT'][h * 128:(h + 1) * 128, :], [], ['sg'])
            if is_ctx:
                self.P.op('dve', lambda e: e.memset(Sf, 0.0), writes=['Sf'])
                self.P.op('dve', lambda e: e.memset(Sb, 0.0), writes=['Sb'])
            else:
                self.dma('sp', Sf, self.s0[h], ['s0'], ['Sf'])
                self.dma('sp', Sb, self.s0[4 + h], ['s0'], ['Sb'])
            TW = min(512, T)
            for (raw, rk, dst, dk) in ((qraw, 'qraw', qr, 'qr'), (kraw, 'kraw', kr, 'kr')):
                for t in range(T // TW):
                    sl = slice(t * TW, (t + 1) * TW)
                    pt, pk = nps()
                    self.mm(pt[:, 0:TW], self.perm, raw[:, sl], True, True, [rk, 'cm'], [pk])
                    t1, t1k = t1rot.next(); t2, t2k = t2rot.next()
                    self.tt('pool', t1[:, 0:TW], raw[:, sl], cosT[:, sl], ALU.mult, [rk, 'cos'], [t1k])
                    self.tt('dve', t2[:, 0:TW], pt[:, 0:TW], sinT[:, sl], ALU.mult, [pk, 'sin'], [t2k])
                    self.tt('dve', dst[:, sl], t1[:, 0:TW], t2[:, 0:TW], ALU.add, [t1k, t2k], [(dk, t)])
            qrk = [('qr', t) for t in range(T // TW)]
            krk = [('kr', t) for t in range(T // TW)]
            if RS < 3: continue
            G = 4
            for n0 in range(0, NCH, G):
                g = min(G, NCH - n0)
                for (dst, QW, nm) in ((qf, QWf, 'qwf'), (qb, QWb, 'qwb')):
                    for n in range(n0, n0 + g):
                        self.tt('pool', dst[:, n, :], qr[:, n * 128:(n + 1) * 128], QW[:, h, :], ALU.mult,
                                qrk + [(nm, h)], [(nm + 'd', n)])
            if RS < 4: continue
            for n in range(NCH):
                pt, pk = nps()
                ptb = pt[:].bitcast(BF16)[:, 0:128]
                self.P.op('pe', lambda e, ptb=ptb, n=n: e.transpose(ptb, kr[:, n * 128:(n + 1) * 128], self.ident),
                          reads=krk + ['cm'], writes=[pk])
                self.act(kft[:, n, :], ptb, AF.Copy, [pk, ('kw', h)], [('kft', n)], scale=kw[:, h * 4:h * 4 + 1])
                self.ts('dve', kbt[:, n, :], ptb, kw[:, h * 4 + 1:h * 4 + 2], None, ALU.mult, None, [pk, ('kw', h)], [('kbt', n)])
            if RS < 5: continue
            for n in range(NCH - 1, -1, -1):
                self.copy('act', Sb_all[:, n, :], Sb, ['Sb'], [('sball', n)])
                pt, pk = nps()
                self.mm(pt[:, 0:128], kbt[:, n, :], vt[:, n, :], True, True, [('kbt', n), 'vt'], [pk])
                self.stt('dve', Sb, Sb, kw[:, h * 4 + 3:h * 4 + 4], pt[:, 0:128], ALU.mult, ALU.add, ['Sb', pk, ('kw', h)], ['Sb'])
            if RS < 6: continue
            for n in range(NCH):
                sl = slice(n * 128, (n + 1) * 128)
                FW = int(_os.environ.get('KDBG_FW', '99'))
                sfb, sfk = Sfb.next()
                self.copy('act', sfb, Sf, ['Sf'], [sfk])
                if FW < 2: continue
                pa, pak = nps()
                self.mm(pa[:, 0:128], kr[:, sl], qr[:, sl], True, True, qrk + krk, [pak])
                sm, smk = smrot.next()
                self.tt('dve', sm, pa[:, 0:128], mask[:, h, :], ALU.mult, [pak, ('mask', h)], [smk])
                if FW < 3: continue
                pr, prk = nps()
                self.mm(pr[:, 0:128], sm, vt[:, n, :], True, False, [smk, 'vt'], [prk])
                self.mm(pr[:, 0:128], qf[:, n, :], sfb, False, False, [('qwfd', n), sfk], [prk])
                self.mm(pr[:, 0:128], qb[:, n, :], Sb_all[:, n, :], False, True, [('qwbd', n), ('sball', n)], [prk])
                if FW < 4: continue
                self.copy('dve', ret_all[:, n, :], pr[:, 0:128], [prk], [('ret', n)])
                self.act(sq_all[:, n, :], ret_all[:, n, :], AF.Square, [('ret', n)], [('sqa', n)])
                if FW < 5: continue
                pd, pdk = nps()
                self.mm(pd[:, 0:128], kft[:, n, :], vt[:, n, :], True, True, [('kft', n), 'vt'], [pdk])
                self.stt('dve', Sf, Sf, kw[:, h * 4 + 2:h * 4 + 3], pd[:, 0:128], ALU.mult, ALU.add, ['Sf', pdk, ('kw', h)], ['Sf'])
            if RS < 7: continue
            if is_ctx:
                self.dma('sp', self.s0[h], Sf, ['Sf'], ['s0'])
                self.dma('sp', self.s0[4 + h], Sb, ['Sb'], ['s0'])
            if RS < 8: continue
            ssk = [('sqa', n) for n in range(NCH)]
            self.P.op('dve', lambda e: e.reduce_sum(ssq, sq_all, axis=AX.X), reads=ssk, writes=['rstd_all'])
            self.rstd(ssq, ssq, 1.0 / 128, ['rstd_all'], 'rstd_all')
            for n in range(NCH):
                rn, rnk = rnrot.next()
                self.act(rn, ret_all[:, n, :], AF.Copy, [('ret', n), 'rstd_all'], [rnk], scale=ssq[:, n:n + 1])
                pt, pk = nps()
                ptb = pt[:].bitcast(BF16)[:, 0:128]
                self.P.op('pe', lambda e, ptb=ptb, rn=rn: e.transpose(ptb, rn, self.ident), reads=[rnk, 'cm'], writes=[pk])
                self.tt('dve', ya[:, n * 128:(n + 1) * 128], ptb, sg[:, n * 128:(n + 1) * 128], ALU.mult, [pk, 'sg'], [('ya', n)])
            self.dma('sp', st['yT'][h * 128:(h + 1) * 128, :], ya, [('ya', n) for n in range(NCH)], ['yT'])
        P.barrier()

    def stage_fourier(self, l, st):
        P = self.P
        T = st['T']; NCH = T // 128
        self.arena_reset()
        fz = self.alloc([128, 4, T], BF16)
        Acs = self.alloc([128, NCH, 4, 256], BF16)
        dfc = self.alloc([128, 256], BF16)
        KT = 256
        trot = self.rot('dt', 2, [128, 2, NCH, KT], BF16)
        self.dma('sp', dfc, self.W['dftc'][:], [], ['dfc'])
        for g in range(4):
            self.dma('sp', fz[:, g, :], st['fT'][g * 128:(g + 1) * 128, :], [], [('fz', g)])
        psi = [0]

        def nps():
            k = psi[0] % 8; psi[0] += 1
            return self.ps[k][:], ('ps', k)
        ei = 0
        for n in range(NCH):
            for g2 in range(2):
                pt, pk = nps()
                for gg in range(2):
                    g = g2 * 2 + gg
                    self.mm(pt[:, gg * 256:(gg + 1) * 256], fz[:, g, n * 128:(n + 1) * 128], dfc, True, True, [('fz', g), 'dfc'], [pk])
                dst = Acs[:, n, g2 * 2:g2 * 2 + 2, :]
                src = pt[:].rearrange("p (a b) -> p a b", a=2)
                self.copy('act' if ei % 2 == 0 else 'dve', dst, src, [pk], [('acs', n)])
                ei += 1
        P.barrier()
        yb = fz
        ack = [('acs', n) for n in range(NCH)]
        for kt in range(T // KT):
            tb_, tk = trot.next()
            for cs in range(2):
                self.dma('sp', tb_[:, cs, :, :], st['dft'][cs][:, kt * KT:(kt + 1) * KT].rearrange("(n p) k -> p n k", p=128), [], [tk])
            for g in range(4):
                pt, pk = nps()
                for n in range(NCH):
                    for cs in range(2):
                        self.mm(pt[:, 0:KT], Acs[:, n, g, cs * 128:(cs + 1) * 128], tb_[:, cs, n, :],
                                n == 0 and cs == 0, n == NCH - 1 and cs == 1, ack + [tk], [pk])
                self.copy('act' if ei % 2 == 0 else 'dve', yb[:, g, kt * KT:(kt + 1) * KT], pt[:, 0:KT], [pk], [('yb', g)])
                ei += 1
        for g in range(4):
            self.dma('sp', st['yT'][512 + g * 128:512 + (g + 1) * 128, :], yb[:, g, :], [('yb', g)], ['yT'])
        P.barrier()

    def stage_sgu(self, l, st):
        P = self.P
        T = st['T']; NCH = T // 128
        self.arena_reset()
        vs = self.alloc([128, NCH, 512], BF16)
        vg = self.alloc([128, NCH, 512], BF16)
        uT = self.alloc([128, 4, T], BF16)
        yc = self.alloc([128, 4, T], BF16)
        wsf = self.alloc([128, 4, 128], F32); wsb = self.alloc([128, 4, 128], BF16)
        bsb = self.alloc([128, 4, 512], F32)
        s1 = self.alloc([128, NCH * 4], F32); s2 = self.alloc([128, NCH * 4], F32); mn = self.alloc([128, NCH * 4], F32)
        PCS = 4
        sqrot = self.rot('sq', 2, [128, PCS * 512], F32)
        tmrot = self.rot('tm', 3, [128, 512], F32)
        self.dma('sp', vs, st['vs_tok'].rearrange("(n p) c -> p n c", p=128), [], ['vs'])
        for g in range(4):
            self.dma('sp', uT[:, g, :], st['uT'][g * 128:(g + 1) * 128, :], [], [('u', g)])
        self.dma('sp', wsf, self.W['sgu_wT'][l], [], ['wsf'])
        self.dma('sp', bsb, self.W['sgu_bb'][l], [], ['bsb'])
        self.copy('dve', wsb, wsf, ['wsf'], ['wsb'])
        for n0 in range(0, NCH, PCS):
            g_ = min(PCS, NCH - n0)
            v3 = vs[:, n0:n0 + g_, :].rearrange("p n (g c) -> p (n g) c", g=4)
            self.P.op('dve', lambda e, v3=v3, n0=n0, g_=g_: e.reduce_sum(s1[:, n0 * 4:(n0 + g_) * 4], v3, axis=AX.X),
                      reads=['vs'], writes=[('s1', n0)])
            sq, sk = sqrot.next()
            sqv = sq[:, 0:g_ * 512]
            self.act(sqv, vs[:, n0:n0 + g_, :].rearrange("p n c -> p (n c)"), AF.Square, ['vs'], [sk])
            self.P.op('dve', lambda e, sqv=sqv, n0=n0, g_=g_: e.reduce_sum(s2[:, n0 * 4:(n0 + g_) * 4],
                                                                           sqv.rearrange("p (a c) -> p a c", c=128), axis=AX.X),
                      reads=[sk], writes=[('s2', n0)])
        s1k = [('s1', n0) for n0 in range(0, NCH, PCS)]; s2k = [('s2', n0) for n0 in range(0, NCH, PCS)]
        self.ts('dve', mn, s1, 1.0 / 128, None, ALU.mult, None, s1k, ['mn'])
        self.tt('dve', s1, mn, mn, ALU.mult, ['mn'], ['msq'])
        self.stt('dve', s2, s2, 1.0 / 128, s1, ALU.mult, ALU.subtract, s2k + ['msq'], ['var'])
        self.rstd(s2, s2, 1.0, ['var'], 'rstdv')
        for n in range(NCH):
            for g in range(4):
                i = n * 4 + g
                self.ts('dve', vg[:, n, g * 128:(g + 1) * 128], vs[:, n, g * 128:(g + 1) * 128],
                        mn[:, i:i + 1], s2[:, i:i + 1], ALU.subtract, ALU.mult, ['vs', 'mn', 'rstdv'], [('vg', n)])
        psi = [0]

        def nps():
            k = psi[0] % 8; psi[0] += 1
            return self.ps[k][:], ('ps', k)
        G = 4
        for g in range(4):
            for n0 in range(0, NCH, G):
                g_ = min(G, NCH - n0)
                pt, pk = nps()
                for n in range(n0, n0 + g_):
                    self.mm(pt[:, (n - n0) * 128:(n - n0 + 1) * 128], vg[:, n, g * 128:(g + 1) * 128], wsb[:, g, :], True, True,
                            [('vg', n), 'wsb'], [pk])
                tm, tk = tmrot.next()
                w_ = g_ * 128
                self.tt('dve', tm[:, 0:w_], pt[:, 0:w_], bsb[:, g, 0:w_], ALU.add, [pk, 'bsb'], [tk])
                self.tt('pool', yc[:, g, n0 * 128:n0 * 128 + w_], tm[:, 0:w_], uT[:, g, n0 * 128:n0 * 128 + w_], ALU.mult,
                        [tk, ('u', g)], [('yc', g)])
            self.dma('sp', st['yT'][1024 + g * 128:1024 + (g + 1) * 128, :], yc[:, g, :], [('yc', g)], ['yT'])
        P.barrier()

    def stage_conv(self, l, st):
        P = self.P
        T = st['T']; RW = st['RW']; NR = T // RW
        self.arena_reset()
        brot = self.rot('cb', 2, [128, T], BF16); crot = self.rot('cc', 2, [128, T], BF16); xrot = self.rot('cx', 2, [128, T], BF16)
        yrot = self.rot('cy', 2, [128, T], F32); orot = self.rot('co', 2, [128, T], F32); drot = self.rot('cd', 2, [128, T], BF16)
        for ct in range(4):
            bt, bk = brot.next(); cg, ck = crot.next(); xv, xk = xrot.next()
            self.dma('sp', bt, st['scT'][ct * 128:(ct + 1) * 128, :], [], [bk])
            self.dma('sp', cg, st['scT'][512 + ct * 128:512 + (ct + 1) * 128, :], [], [ck])
            self.dma('sp', xv, st['scT'][1024 + ct * 128:1024 + (ct + 1) * 128, :], [], [xk])
            y, yk = yrot.next(); o, ok = orot.next(); d, dk = drot.next()
            w0 = self.cw[:, (l * 4 + ct) * 3 + 0:(l * 4 + ct) * 3 + 1]
            w1 = self.cw[:, (l * 4 + ct) * 3 + 1:(l * 4 + ct) * 3 + 2]
            w2 = self.cw[:, (l * 4 + ct) * 3 + 2:(l * 4 + ct) * 3 + 3]
            self.tt('dve', y, cg, xv, ALU.mult, [ck, xk], [yk])
            self.act(o, y, AF.Copy, [yk, 'cw'], [ok], scale=w1)
            y3 = y.rearrange("p (r w) -> p r w", w=RW); o3 = o.rearrange("p (r w) -> p r w", w=RW)
            self.stt('dve', o3[:, :, 1:RW], y3[:, :, 0:RW - 1], w0, o3[:, :, 1:RW], ALU.mult, ALU.add, [yk, ok, 'cw'], [ok])
            self.stt('dve', o3[:, :, 0:RW - 1], y3[:, :, 1:RW], w2, o3[:, :, 0:RW - 1], ALU.mult, ALU.add, [yk, ok, 'cw'], [ok])
            self.tt('dve', d, o, bt, ALU.mult, [ok, bk], [dk])
            self.dma('sp', st['yT'][1536 + ct * 128:1536 + (ct + 1) * 128, :], d, [dk], ['yT'])
        P.barrier()

    def stage_merge(self, l, st):
        P = self.P
        T = st['T']; TW = min(512, T)
        self.arena_reset()
        wbr = self.alloc([128, 16, D], BF16)
        for n in range(4):
            self.dma('pool', wbr[:, n * 4:(n + 1) * 4, :], self.W['w_branch'][l][n * 512:(n + 1) * 512, :].rearrange("(k p) d -> p k d", p=128),
                     [], [('wbr', n)])
        wbk = [('wbr', n) for n in range(4)]
        yrot = self.rot('yt', 2, [128, 16, TW], BF16)
        grot = self.rot('gt', 3, [128, 4, TW], BF16)
        trot = self.rot('t', 12, [128, TW], BF16)
        mrot = self.rot('mg', 2, [128, 16, TW], BF16)
        psi = [0]

        def nps():
            k = psi[0] % 8; psi[0] += 1
            return self.ps[k][:], ('ps', k)
        for t in range(T // TW):
            sl = slice(t * TW, (t + 1) * TW)
            yt, yk = yrot.next()
            self.dma('sp', yt, st['yT'][:, sl].rearrange("(k p) n -> p k n", p=128), ['yT'], [yk])
            mg, mk = mrot.next()
            pend = []

            def flush(keep):
                while len(pend) > keep:
                    tl_, d_ = pend.pop(0)
                    pa_, pak_ = nps()
                    for n_ in range(4):
                        self.mm(pa_[:, 0:TW], self.ident, tl_[n_][0], n_ == 0, n_ == 3, [tl_[n_][1], 'cm'], [pak_])
                    self.act(mg[:, d_, :], pa_[:, 0:TW], AF.Copy, [pak_], [mk])
            for dt_ in range(16):
                gt, gk = grot.next()
                self.dma('sp', gt, st['gateT'][:, sl].rearrange("(n r) t -> r n t", n=4)[dt_ * 128:(dt_ + 1) * 128], [], [gk])
                tl = []
                for n in range(4):
                    pt, pk = nps()
                    for mc in range(4):
                        self.mm(pt[:, 0:TW], wbr[:, n * 4 + mc, dt_ * 128:(dt_ + 1) * 128], yt[:, n * 4 + mc, :],
                                mc == 0, mc == 3, wbk + [yk], [pk])
                    tm, tk = trot.next()
                    self.tt('dve', tm, pt[:, 0:TW], gt[:, n, :], ALU.mult, [pk, gk], [tk])
                    tl.append((tm, tk))
                flush(0)
                pend.append((tl, dt_))
            flush(0)
            self.dma('sp', st['mergedT'][:, sl].rearrange("(k p) n -> p k n", p=128), mg, [mk], ['mergedT'])
        P.barrier()

    def stage_outproj(self, l, st, which):
        P = self.P
        T = st['T']; TW = min(512, T); j = st['j']
        if which == 'mix':
            K = D; wsrc = self.W['w_out'][l]; src = st['mergedT']; G = self.modv[:, 2, :, j]
            res_in = st['res_in']
        else:
            K = DFF; wsrc = self.W['ffn_w_out'][l]; src = st['hT']; G = self.modv[:, 5, :, j]
            res_in = st['res']
        res_out = st['res']
        NKT = K // 128
        self.arena_reset()
        big = (K != D)
        inrot = self.rot('in', 2, [128, NKT, TW], BF16)
        if big:
            wrot = self.rot('w', 3, [128, NKT, 128], BF16)
        else:
            wres = self.alloc([128, 16, D], BF16)
            for dt_ in range(16):
                self.dma('pool', wres[:, dt_, :], wsrc[dt_], [], [('wres', dt_)], max_dma_last_dim=8192)
        mix = self.alloc([128, 16, TW], F32)
        rrot = self.rot('res', 1, [128, 16, TW], F32)
        sqrot = self.rot('sq', 3, [128, TW], BF16)
        rs = self.alloc([128, TW], F32)
        tmrot = self.rot('tm', 2 if big else 3, [128, TW], F32)
        psi = [0]

        def nps():
            k = 1 + psi[0] % 7; psi[0] += 1
            return self.ps[k][:], ('ps', k)
        nb = T // TW

        def load_in(t):
            sl = slice(t * TW, (t + 1) * TW)
            it, ik = inrot.next()
            self.dma('sp', it, src[:, sl].rearrange("(k p) n -> p k n", p=128), [('src', t)], [ik])
            return it, ik

        def load_res(t):
            sl = slice(t * TW, (t + 1) * TW)
            rt, rk = rrot.next()
            self.dma('sp', rt, res_in[:, sl].rearrange("(k p) n -> p k n", p=128), [('res', t)], [rk])
            return rt, rk
        cur_in = load_in(0); cur_res = load_res(0)
        for t in range(nb):
            sl = slice(t * TW, (t + 1) * TW)
            it, ik = cur_in; rt, rk = cur_res
            ss = self.ps[0][:, 0:TW]
            pend = []

            def flush(keep):
                while len(pend) > keep:
                    sq_, sk_, d_ = pend.pop(0)
                    self.mm(ss, self.ones, sq_, d_ == 0, d_ == 15, [sk_, 'cm'], ['ps0'])
            for dt_ in range(16):
                pt, pk = nps()
                if big:
                    wt, wk = wrot.next()
                    self.dma('pool', wt.rearrange("p k n -> p (k n)"), self.wo_bf[l, dt_], [], [wk])
                    for k in range(NKT):
                        self.mm(pt[:, 0:TW], wt[:, k, :], it[:, k, :], k == 0, k == NKT - 1, [wk, ik], [pk])
                else:
                    for k in range(NKT):
                        self.mm(pt[:, 0:TW], wres[:, dt_, k * 128:(k + 1) * 128], it[:, k, :], k == 0, k == NKT - 1,
                                [('wres', dt_), ik], [pk])
                flush(1)
                self.copy('dve', mix[:, dt_, :], pt[:, 0:TW], [pk], [('mix', dt_)])
                sq, sk = sqrot.next()
                self.act(sq, mix[:, dt_, :], AF.Square, [('mix', dt_)], [sk])
                pend.append((sq, sk, dt_))
            flush(0)
            if t + 1 < nb:
                cur_in = load_in(t + 1)
            self.rstd(rs, ss, 1.0 / D, ['ps0'], 'rs')
            for dt_ in range(16):
                tm, tk = tmrot.next()
                self.stt('dve', tm, mix[:, dt_, :], G[:, dt_:dt_ + 1], rs, ALU.mult, ALU.mult, [('mix', dt_), 'rs', 'modv'], [tk])
                self.tt('dve', rt[:, dt_, :], rt[:, dt_, :], tm, ALU.add, [rk, tk], [rk])
            self.dma('sp', res_out[:, sl].rearrange("(k p) n -> p k n", p=128), rt, [rk], [('res', t)])
            if t + 1 < nb:
                cur_res = load_res(t + 1)
        P.barrier()


def _bf(a):
    return np.asarray(a, dtype=np.float32).astype(ml_dtypes.bfloat16)


def _rope_tables(T, latent):
    cos = np.ones((128, T), np.float32); sin = np.zeros((128, T), np.float32)
    if latent:
        t = np.arange(T)
        n = 64
        freqs = (np.float32(10000.0) ** (-np.arange(0, n, 2, dtype=np.float32) / np.float32(n))).astype(np.float32)
        for half, pos in ((0, t // 64), (1, t % 64)):
            ang = pos.astype(np.float32)[None, :] * freqs[:, None]
            c = np.cos(ang).astype(np.float32); s = np.sin(ang).astype(np.float32)
            b = half * 64
            cos[b:b + 32] = c; cos[b + 32:b + 64] = c
            sin[b:b + 32] = -s; sin[b + 32:b + 64] = s
    return np.stack([cos, sin]).astype(np.float32)


def _dft_tables(T):
    k = np.arange(T, dtype=np.int64)
    ang = 2.0 * np.pi * ((k[:, None] * k[None, :]) % T).astype(np.float64) / T
    sc = 1.0 / np.sqrt(T * 128.0)
    return np.stack([_bf(np.cos(ang) * sc), _bf(-np.sin(ang) * sc)])


def _consts():
    C = 128
    i = np.arange(C, dtype=np.float32)
    rel = i[None, :] - i[:, None]
    s = np.float32(128 ** -0.5)
    dconst = np.zeros((128, 6 * 128 + 4), np.float32)
    dconst[:, 0:128] = np.maximum(rel, 0)
    dconst[:, 128:256] = (rel >= 0).astype(np.float32) * s
    dconst[:, 256:384] = np.maximum(-rel, 0)
    dconst[:, 384:512] = (rel <= 0).astype(np.float32) * s
    dconst[:, 512:640] = (i + 1)[None, :]
    dconst[:, 640:768] = (C - i)[None, :]
    dconst[:, 768] = C - 1 - i
    dconst[:, 769] = i
    dconst[:, 770] = C
    ident = np.eye(128, dtype=np.float32)
    perm = np.zeros((128, 128), np.float32)
    for d in range(128):
        partner = d + 32 if (d % 64) < 32 else d - 32
        perm[partner, d] = 1.0
    cmat = _bf(np.concatenate([ident, perm, np.ones((128, 128), np.float32)], axis=1))
    c = np.arange(128, dtype=np.int64)
    ang = 2.0 * np.pi * ((c[:, None] * c[None, :]) % 128).astype(np.float64) / 128
    dftc = _bf(np.concatenate([np.cos(ang), np.sin(ang)], axis=1))
    return dconst, cmat, dftc


def prepare_inputs(x, c, ctx, c_ctx, ada_w, ada_b, norm_g, w_in, ret_log_decay, conv_w, sgu_w, sgu_b,
                   w_branch, w_out, ffn_w_in, ffn_w_out, cores=range(NCORES), depth=DEPTH):
    f = lambda a: np.ascontiguousarray(np.asarray(a, dtype=np.float32))
    x = np.asarray(x); ctx = np.asarray(ctx); c = np.asarray(c); c_ctx = np.asarray(c_ctx)
    dconst, cmat, dftc = _consts()
    shared = {
        "ada_w": f(np.asarray(ada_w)[:depth]),
        "ada_b": f(np.asarray(ada_b).reshape(DEPTH, 96, 128).transpose(0, 2, 1)[:depth]),
        "norm_g": f(np.asarray(norm_g).reshape(DEPTH, 4, 16, 128).transpose(3, 0, 1, 2).reshape(128, -1)),
        "w_in": f(np.asarray(w_in)[:depth]),
        "lgb": f(np.broadcast_to(np.asarray(ret_log_decay).reshape(1, DEPTH * 8), (128, DEPTH * 8))),
        "conv_w": f(np.asarray(conv_w).reshape(DEPTH, 3, 4, 128).transpose(3, 0, 2, 1).reshape(128, -1)),
        "sgu_wT": f(np.asarray(sgu_w).transpose(0, 3, 1, 2)[:depth]),
        "sgu_bb": f(np.broadcast_to(np.asarray(sgu_b)[:, None, :, None, :], (DEPTH, 128, 4, 4, 128)).reshape(DEPTH, 128, 4, 512)[:depth]),
        "w_branch": f(np.asarray(w_branch).reshape(DEPTH, 4 * MIXW, D)[:depth]),
        "w_out": f(np.asarray(w_out)[:depth].reshape(depth, 16, 128, 16, 128).transpose(0, 3, 2, 1, 4).reshape(depth, 16, 128, D)),
        "ffn_w_in": f(np.asarray(ffn_w_in)[:depth]),
        "ffn_w_out": f(np.asarray(ffn_w_out)[:depth].reshape(depth, 44, 128, 16, 128).transpose(0, 3, 2, 1, 4).reshape(depth, 16, 128, DFF)),
        "dconst": dconst, "cmat": cmat, "dftc": dftc,
        "rope_l": _rope_tables(SEQ, True), "rope_c": _rope_tables(CTX, False),
        "dft_l": _dft_tables(SEQ), "dft_cx": _dft_tables(CTX),
    }
    in_maps = []
    for b in cores:
        m = dict(shared)
        m["xT"] = f(x[b].T)
        m["ctxT"] = f(ctx[b].T)
        cc2 = np.stack([c[b], c_ctx], axis=1).astype(np.float32)
        m["cc"] = f(cc2.reshape(16, 128, 2).transpose(1, 0, 2))
        in_maps.append(m)
    return in_maps


_NC_CACHE = {}


def kernel(**inputs):
    if 'nc' not in _NC_CACHE:
        _NC_CACHE['nc'] = Builder().build()
    nc = _NC_CACHE['nc']
    in_maps = prepare_inputs(**inputs)
    res = run_bass_kernel_spmd(nc, in_maps, core_ids=list(range(NCORES)))
    out = np.stack([np.asarray(r["outT"]).T for r in res.results], axis=0)
    return np.ascontiguousarray(out.astype(np.float32))
```
